# Optimizing a Trainium2 kernel written in Bass

```python
import jax, jax.numpy as jnp
from jax import lax
import numpy as np

D_MODEL = 2048
BATCH = 4
SEQ = 2048
DEPTH = 4
DEC_BATCH = 128
DEC_SEQ = 4
PAST_LEN = 16384
PAGE_SIZE = 128

D_MIX = D_MODEL
N_MIXERS = 4
D_GROUP = D_MIX // N_MIXERS
N_SUB = 4
D_SUB = D_GROUP // N_SUB
D_IN = 8 * D_GROUP
CHUNK = 128
CONV_B_WIDTH = 31
POOL_WINDOWS = (2, 4, 8, 16)
POOL_PREV = max(POOL_WINDOWS) - 1
SCONV_WIDTH = 3
FFN_CONV_WIDTH = 3
D_FF = ((8 * D_MODEL // 3 + 127) // 128) * 128
N_MEM = 256
N_XHEADS = 4
D_XHEAD = 128
D_X = N_XHEADS * D_XHEAD
EPS = 1e-6

kernel_name = "hybrid_parallel_group_decoder_step"


def rms_norm(x, g):
    xf = x.astype(jnp.float32)
    y = xf * lax.rsqrt(jnp.mean(xf * xf, axis=-1, keepdims=True) + EPS)
    return (y * g.astype(jnp.float32)).astype(x.dtype)


def layer_norm(x, g):
    xf = x.astype(jnp.float32)
    mu = jnp.mean(xf, axis=-1, keepdims=True)
    var = jnp.mean(jnp.square(xf - mu), axis=-1, keepdims=True)
    return ((xf - mu) * lax.rsqrt(var + EPS) * g.astype(jnp.float32)).astype(x.dtype)


def causal_dwconv(x_ext, w):
    c = x_ext.shape[-1]
    return lax.conv_general_dilated(
        x_ext, w[:, None, :].astype(x_ext.dtype), window_strides=(1,), padding='VALID',
        dimension_numbers=('NWC', 'WIO', 'NWC'), feature_group_count=c)


def chunk_gating_mixer(u, v, g_v, w_s, b_s):
    bn, t, _ = u.shape
    lc = min(t, CHUNK)
    vn = layer_norm(v, g_v)
    mask = jnp.tril(jnp.ones((lc, lc), dtype=bool))
    w = jnp.where(mask[None], w_s[:, :lc, :lc], 0).astype(vn.dtype)
    vc = vn.reshape(bn, t // lc, lc, N_SUB, D_SUB)
    bias = jnp.transpose(b_s[:, :lc])[None, None, :, :, None].astype(vn.dtype)
    z = jnp.einsum('hij,bcjhd->bcihd', w, vc) + bias
    y = u.reshape(bn, t // lc, lc, N_SUB, D_SUB) * z
    return y.reshape(bn, t, D_GROUP), vn


def conformer_conv_mixer(a, gate, prev, w_conv, b_conv, gn_g, gn_b):
    bn, t, _ = a.shape
    h = a * jax.nn.sigmoid(gate)
    ext = jnp.concatenate([prev.astype(h.dtype), h], axis=1)
    y = causal_dwconv(ext, w_conv) + b_conv
    yf = y.astype(jnp.float32).reshape(bn, t, N_SUB, D_SUB)
    mu = jnp.mean(yf, axis=-1, keepdims=True)
    var = jnp.mean(jnp.square(yf - mu), axis=-1, keepdims=True)
    yf = ((yf - mu) * lax.rsqrt(var + EPS)).reshape(bn, t, D_GROUP)
    yf = yf * gn_g.astype(jnp.float32) + gn_b.astype(jnp.float32)
    return jax.nn.silu(yf).astype(a.dtype), ext[:, -(CONV_B_WIDTH - 1):]


def multiscale_pool_mixer(x, prev, start_pos, w_lin, scale):
    bn, t, _ = x.shape
    p = POOL_PREV
    ext = jnp.concatenate([prev.astype(x.dtype), x], axis=1)
    ext_f = ext.astype(jnp.float32).reshape(bn, p + t, N_SUB, D_SUB)
    cs = jnp.concatenate([jnp.zeros((bn, 1, N_SUB, D_SUB), jnp.float32),
                          jnp.cumsum(ext_f, axis=1)], axis=1)
    pos = start_pos + jnp.arange(t)
    means = []
    for g, win in enumerate(POOL_WINDOWS):
        hi = cs[:, p + 1:p + 1 + t, g]
        lo = cs[:, p + 1 - win:p + 1 - win + t, g]
        cnt = jnp.minimum(pos + 1, win).astype(jnp.float32)
        means.append((hi - lo) / cnt[None, :, None])
    pooled = jnp.stack(means, axis=2) - ext_f[:, p:]
    y = jnp.einsum('btgc,gcd->btgd', pooled.astype(x.dtype), w_lin).reshape(bn, t, D_GROUP)
    return y * scale, ext[:, -p:]


def short_gated_conv_mixer(xt, bg, cg, prev, w_conv):
    h = cg * xt
    ext = jnp.concatenate([prev.astype(h.dtype), h], axis=1)
    return bg * causal_dwconv(ext, w_conv), ext[:, -(SCONV_WIDTH - 1):]


def memory_kv(mem, g_mem, w_k, w_v):
    m = rms_norm(mem, g_mem)
    return m @ w_k, m @ w_v


def memory_cross_attention(x, mem_k, mem_v, w_q, w_o):
    bn, t, _ = x.shape
    nm = mem_k.shape[1]
    q = (x @ w_q).reshape(bn, t, N_XHEADS, D_XHEAD)
    kh = mem_k.astype(q.dtype).reshape(bn, nm, N_XHEADS, D_XHEAD)
    vh = mem_v.astype(q.dtype).reshape(bn, nm, N_XHEADS, D_XHEAD)
    s = jnp.einsum('bthd,bmhd->bhtm', q, kh).astype(jnp.float32) * (D_XHEAD ** -0.5)
    pr = jax.nn.softmax(s, axis=-1).astype(vh.dtype)
    o = jnp.einsum('bhtm,bmhd->bthd', pr, vh).reshape(bn, t, D_X)
    return o @ w_o


def conv_ffn(x, prev, w_up, w_conv, w_down):
    h = x @ w_up
    ext = jnp.concatenate([prev.astype(h.dtype), h], axis=1)
    hc = causal_dwconv(ext, w_conv)
    g, u = jnp.split(hc, 2, axis=-1)
    return (jax.nn.silu(g) * u) @ w_down, ext[:, -(FFN_CONV_WIDTH - 1):]


def trunk_layer(h, mem_k, mem_v, prev_b, prev_pool, prev_sc, prev_ffn, start_pos, p):
    xn = rms_norm(h, p['g_mix_pre'])
    z = xn @ p['w_in']
    a_u, a_v, b_a, b_g, c_x, d_x, d_b, d_c = jnp.split(z, 8, axis=-1)
    y_a, v_rows = chunk_gating_mixer(a_u, a_v, p['a_norm_g'], p['a_ws'], p['a_bs'])
    y_b, new_b = conformer_conv_mixer(b_a, b_g, prev_b, p['b_conv_w'], p['b_conv_b'], p['b_gn_g'], p['b_gn_b'])
    y_c, new_pool = multiscale_pool_mixer(c_x, prev_pool, start_pos, p['c_lin'], p['c_scale'])
    y_d, new_sc = short_gated_conv_mixer(d_x, d_b, d_c, prev_sc, p['d_conv_w'])
    mix = jnp.concatenate([y_a, y_b, y_c, y_d], axis=-1) @ p['w_out']
    h = h + rms_norm(mix, p['g_mix_post'])
    xa = memory_cross_attention(rms_norm(h, p['g_x_pre']), mem_k, mem_v, p['w_xq'], p['w_xo'])
    h = h + rms_norm(xa, p['g_x_post'])
    f, new_ffn = conv_ffn(rms_norm(h, p['g_ffn_pre']), prev_ffn, p['w_up'], p['f_conv_w'], p['w_down'])
    h = h + rms_norm(f, p['g_ffn_post'])
    return h, new_b, new_pool, new_sc, new_ffn, v_rows


def setup_inputs(seed: int = 0) -> dict:
    key = jax.random.key(seed)
    ks = iter(jax.random.split(key, 48))

    def nrm(shape, scale=1.0):
        return jax.random.normal(next(ks), shape, jnp.float32) * scale

    def gain(shape):
        return 1.0 + nrm(shape, 0.02)

    L = DEPTH
    return {
        'x_prompt': nrm((BATCH, SEQ, D_MODEL)),
        'x_sample': nrm((DEC_BATCH, DEC_SEQ, D_MODEL)),
        'cache_mem_k': nrm((L, DEC_BATCH, N_MEM, D_X)),
        'cache_mem_v': nrm((L, DEC_BATCH, N_MEM, D_X)),
        'state_conv_b': nrm((L, DEC_BATCH, CONV_B_WIDTH - 1, D_GROUP), 0.5),
        'state_pool': nrm((L, DEC_BATCH, POOL_PREV, D_GROUP)),
        'state_sconv': nrm((L, DEC_BATCH, SCONV_WIDTH - 1, D_GROUP), 0.5),
        'state_ffn_conv': nrm((L, DEC_BATCH, FFN_CONV_WIDTH - 1, 2 * D_FF)),
        'mem_prompt': nrm((BATCH, N_MEM, D_MODEL)),
        'g_mix_pre': gain((L, D_MODEL)),
        'g_mix_post': gain((L, D_MODEL)),
        'g_mem': gain((L, D_MODEL)),
        'g_x_pre': gain((L, D_MODEL)),
        'g_x_post': gain((L, D_MODEL)),
        'g_ffn_pre': gain((L, D_MODEL)),
        'g_ffn_post': gain((L, D_MODEL)),
        'w_in': nrm((L, D_MODEL, D_IN), D_MODEL ** -0.5),
        'w_out': nrm((L, D_MIX, D_MODEL), D_MIX ** -0.5),
        'a_norm_g': gain((L, D_GROUP)),
        'a_ws': nrm((L, N_SUB, CHUNK, CHUNK), CHUNK ** -0.5),
        'a_bs': 1.0 + nrm((L, N_SUB, CHUNK), 0.01),
        'b_conv_w': nrm((L, CONV_B_WIDTH, D_GROUP), CONV_B_WIDTH ** -0.5),
        'b_conv_b': nrm((L, D_GROUP), 0.02),
        'b_gn_g': gain((L, D_GROUP)),
        'b_gn_b': nrm((L, D_GROUP), 0.02),
        'c_lin': nrm((L, N_SUB, D_SUB, D_SUB), D_SUB ** -0.5),
        'c_scale': gain((L, D_GROUP)),
        'd_conv_w': nrm((L, SCONV_WIDTH, D_GROUP), SCONV_WIDTH ** -0.5),
        'w_xq': nrm((L, D_MODEL, D_X), D_MODEL ** -0.5),
        'w_xk': nrm((L, D_MODEL, D_X), D_MODEL ** -0.5),
        'w_xv': nrm((L, D_MODEL, D_X), D_MODEL ** -0.5),
        'w_xo': nrm((L, D_X, D_MODEL), D_X ** -0.5),
        'w_up': nrm((L, D_MODEL, 2 * D_FF), D_MODEL ** -0.5),
        'f_conv_w': nrm((L, FFN_CONV_WIDTH, 2 * D_FF), FFN_CONV_WIDTH ** -0.5),
        'w_down': nrm((L, D_FF, D_MODEL), D_FF ** -0.5),
    }


def reference(x_prompt, x_sample, cache_mem_k, cache_mem_v, state_conv_b, state_pool, state_sconv,
              state_ffn_conv, mem_prompt, g_mix_pre, g_mix_post, g_mem, g_x_pre, g_x_post, g_ffn_pre,
              g_ffn_post, w_in, w_out, a_norm_g, a_ws, a_bs, b_conv_w, b_conv_b, b_gn_g, b_gn_b, c_lin,
              c_scale, d_conv_w, w_xq, w_xk, w_xv, w_xo, w_up, f_conv_w, w_down):
    bp = x_prompt.shape[0]
    dt = x_prompt.dtype
    zb = jnp.zeros((bp, CONV_B_WIDTH - 1, D_GROUP), dt)
    zpool = jnp.zeros((bp, POOL_PREV, D_GROUP), dt)
    zsc = jnp.zeros((bp, SCONV_WIDTH - 1, D_GROUP), dt)
    zffn = jnp.zeros((bp, FFN_CONV_WIDTH - 1, 2 * D_FF), dt)

    hp = x_prompt
    hs = x_sample
    p_mk, p_mv, p_b, p_pool, p_sc, p_ffn = [], [], [], [], [], []
    s_b, s_pool, s_sc, s_ffn, s_v = [], [], [], [], []
    for l in range(DEPTH):
        prm = {
            'g_mix_pre': g_mix_pre[l], 'g_mix_post': g_mix_post[l],
            'g_x_pre': g_x_pre[l], 'g_x_post': g_x_post[l],
            'g_ffn_pre': g_ffn_pre[l], 'g_ffn_post': g_ffn_post[l],
            'w_in': w_in[l], 'w_out': w_out[l],
            'a_norm_g': a_norm_g[l], 'a_ws': a_ws[l], 'a_bs': a_bs[l],
            'b_conv_w': b_conv_w[l], 'b_conv_b': b_conv_b[l], 'b_gn_g': b_gn_g[l], 'b_gn_b': b_gn_b[l],
            'c_lin': c_lin[l], 'c_scale': c_scale[l], 'd_conv_w': d_conv_w[l],
            'w_xq': w_xq[l], 'w_xo': w_xo[l],
            'w_up': w_up[l], 'f_conv_w': f_conv_w[l], 'w_down': w_down[l],
        }
        mk, mv = memory_kv(mem_prompt, g_mem[l], w_xk[l], w_xv[l])
        hp, nb, npool, nsc, nffn, _ = trunk_layer(hp, mk, mv, zb, zpool, zsc, zffn, 0, prm)
        p_mk.append(mk); p_mv.append(mv); p_b.append(nb); p_pool.append(npool)
        p_sc.append(nsc); p_ffn.append(nffn)
        hs, sb, spool, ssc, sffn, sv = trunk_layer(
            hs, cache_mem_k[l], cache_mem_v[l], state_conv_b[l], state_pool[l], state_sconv[l],
            state_ffn_conv[l], PAST_LEN, prm)
        s_b.append(sb); s_pool.append(spool); s_sc.append(ssc); s_ffn.append(sffn); s_v.append(sv)

    return (hp, hs,
            jnp.stack(p_mk), jnp.stack(p_mv), jnp.stack(p_b), jnp.stack(p_pool), jnp.stack(p_sc), jnp.stack(p_ffn),
            jnp.stack(s_b), jnp.stack(s_pool), jnp.stack(s_sc), jnp.stack(s_ffn), jnp.stack(s_v))
```

```python
import contextlib
import numpy as np
import concourse.bass as bass
import concourse.mybir as mybir
from concourse.bass_utils import run_bass_kernel_spmd

F32 = mybir.dt.float32
BF16 = mybir.dt.bfloat16
AF = mybir.ActivationFunctionType
ALU = mybir.AluOpType
AX = mybir.AxisListType

L = 4
D = 2048
KC = 16
DIN = 4096
DG = 512
DFF = 5504
NFF = 43
NMEM = 256
DX = 512
TB = 768
SB = 64
TMAX = TB + SB
REG = 1536
HALO = 512
OWN = 1024
NB = 16
EPS = 1e-6
QSCALE = 128 ** -0.5
SELF_WAIT = True


class StopEmit(Exception):
    pass


class Sem:
    def __init__(self, h):
        self.h = h
        self.v = 0


class Cell:
    __slots__ = ("lw", "rd")

    def __init__(self):
        self.lw = None
        self.rd = {}


class View:
    def __init__(self, ap, cells):
        self.ap = ap
        self.cells = cells


class Region:
    def __init__(self, nc, es, name, nbytes, cell_bytes, psum=False):
        self.nbytes = nbytes
        self.cb = cell_bytes
        if psum:
            self.t = es.enter_context(nc.psum_tensor(name, [128, nbytes // 4], F32))
        else:
            self.t = es.enter_context(nc.sbuf_tensor(name, [128, nbytes // 4], F32))
        self.cells = [Cell() for _ in range((nbytes + cell_bytes - 1) // cell_bytes)]

    def view(self, off, dtype, shape, parts=128):
        n = int(np.prod(shape))
        esz = 2 if dtype == BF16 else 4
        assert off % 4 == 0 and off + n * esz <= self.nbytes, (off, n, esz, self.nbytes)
        w0 = off // 4
        w1 = (off + n * esz + 3) // 4
        ap = self.t[0:parts, w0:w1]
        if dtype == BF16:
            ap = ap.bitcast(BF16)[:, 0:n]
        if len(shape) == 2:
            ap = ap.rearrange("p (a b) -> p a b", a=shape[0])
        elif len(shape) == 3:
            ap = ap.rearrange("p (a b c) -> p a b c", a=shape[0], b=shape[1])
        c0 = off // self.cb
        c1 = (off + n * esz - 1) // self.cb
        return View(ap, self.cells[c0:c1 + 1])


def bc_mid(ap2, n):
    a = ap2.ap
    return bass.AP(ap2.tensor, ap2.offset, [list(a[0]), [0, n], list(a[1])])


def bc_last(ap2, n):
    a = ap2.ap
    return bass.AP(ap2.tensor, ap2.offset, [list(a[0]), list(a[1]), [0, n]])


class Tracker:
    NDMA = 40

    def __init__(self, nc, es):
        self.nc = nc
        self.plan = False
        self.engs = {}
        for nm, e in (("p", nc.tensor), ("a", nc.scalar), ("v", nc.vector), ("g", nc.gpsimd), ("s", nc.sync)):
            self.engs[nm] = dict(eng=e, sem=Sem(es.enter_context(nc.semaphore("sem_" + nm))), waited={}, pend=False)
        self.dsl = [Sem(es.enter_context(nc.semaphore("dsem%d" % i))) for i in range(self.NDMA)]
        self.dnext = 0
        self.nops = 0

    @staticmethod
    def _cells(lst):
        out = []
        for x in lst:
            if isinstance(x, View):
                out.extend(x.cells)
            elif isinstance(x, Cell):
                out.append(x)
            else:
                out.extend(x)
        return out

    def _deps(self, E, rc, wc):
        deps = {}

        def need(sv):
            s, v = sv
            if deps.get(s, 0) < v:
                deps[s] = v
        for c in rc:
            if c.lw is not None:
                need(c.lw)
        for c in wc:
            if c.lw is not None:
                need(c.lw)
            for s, v in c.rd.items():
                need((s, v))
        pend = []
        for s, v in deps.items():
            if s is E["sem"]:
                if E is self.engs["p"] or not SELF_WAIT:
                    continue
                if v > s.v:
                    continue
            if E["waited"].get(s, 0) < v:
                pend.append((s, v))
        return pend

    def op(self, en, method, R, W, *args, inc=True, **kw):
        if self.plan:
            return None
        E = self.engs[en]
        rc = self._cells(R)
        wc = self._cells(W)
        pend = self._deps(E, rc, wc)
        for s, v in pend[:-1]:
            E["eng"].wait_ge(s.h, v)
        ins = getattr(E["eng"], method)(*args, **kw)
        if pend:
            ins._wait_ge(pend[-1][0].h, pend[-1][1])
        for s, v in pend:
            E["waited"][s] = v
        sem = E["sem"]
        tag = (sem, sem.v + 1)
        if inc:
            ins.then_inc(sem.h, 1)
            sem.v += 1
            E["pend"] = False
        else:
            E["pend"] = True
        for c in rc:
            c.rd[tag[0]] = tag[1]
        for c in wc:
            c.lw = tag
            c.rd = {}
        self.nops += 1
        return ins

    def dma(self, qn, out, in_, R, W, **kw):
        if self.plan:
            return None
        E = self.engs[qn]
        rc = self._cells(R)
        wc = self._cells(W)
        pend = self._deps(E, rc, wc)
        slot = self.dsl[self.dnext]
        self.dnext = (self.dnext + 1) % self.NDMA
        if slot.v > 0 and E["waited"].get(slot, 0) < slot.v:
            pend.append((slot, slot.v))
        for s, v in pend[:-1]:
            E["eng"].wait_ge(s.h, v)
        ins = E["eng"].dma_start(out=out, in_=in_, **kw)
        if pend:
            ins._wait_ge(pend[-1][0].h, pend[-1][1])
        for s, v in pend:
            E["waited"][s] = v
        ins.then_inc(slot.h, 16)
        slot.v += 16
        tag = (slot, slot.v)
        for c in rc:
            c.rd[slot] = slot.v
        for c in wc:
            c.lw = tag
            c.rd = {}
        return ins

    def finish(self, out_cells):
        S = self.engs["s"]
        for slot in self.dsl:
            if slot.v > 0:
                S["eng"].wait_ge(slot.h, slot.v)
        for nm in ("p", "a", "v", "g"):
            E = self.engs[nm]
            assert not E["pend"], nm
            if E["sem"].v > 0:
                S["eng"].wait_ge(E["sem"].h, E["sem"].v)


def build(depth=L, nblocks=2, stop=None, dbg_shape=None, LW=L):
    nc = bass.Bass("TRN2", target_bir_lowering=False)
    L = LW
    es = contextlib.ExitStack()
    T = Tracker(nc, es)

    def din(name, shape, dt=F32):
        return nc.dram_tensor(name, list(shape), dt, kind="ExternalInput").ap()

    def dout(name, shape, dt=F32):
        return nc.dram_tensor(name, list(shape), dt, kind="ExternalOutput").ap()

    xreg = din("xreg", [REG, D])
    xsmp = din("xsmp", [SB, D])
    memp = din("memp", [NMEM, D])
    ck = din("ck", [L, NB, NMEM, DX])
    cv = din("cv", [L, NB, NMEM, DX])
    scb = din("scb", [L, NB, 30, DG])
    spl = din("spl", [L, NB, 15, DG])
    ssc = din("ssc", [L, NB, 2, DG])
    sff = din("sff", [L, NB, 2, 2 * DFF])
    maskb = din("maskb", [128, HALO])
    invc = din("invc", [128, 4, 16])
    tri = din("tri", [128, 128])
    ident = din("ident", [128, 128])
    gn = {}
    for nm in ("g_mix_pre", "g_mix_post", "g_mem", "g_x_pre", "g_x_post", "g_ffn_pre", "g_ffn_post"):
        gn[nm] = din(nm, [L, D])
    w_in = din("w_in", [L, D, DIN])
    w_out = din("w_out", [L, D, D])
    a_norm_g = din("a_norm_g", [L, DG])
    a_ws = din("a_ws", [L, 4, 128, 128])
    a_bs = din("a_bs", [L, 4, 128])
    b_conv_w = din("b_conv_w", [L, 31, DG])
    b_conv_b = din("b_conv_b", [L, DG])
    b_gn_g = din("b_gn_g", [L, DG])
    b_gn_b = din("b_gn_b", [L, DG])
    c_lin = din("c_lin", [L, 4, 128, 128])
    c_scale = din("c_scale", [L, DG])
    d_conv_w = din("d_conv_w", [L, 3, DG])
    w_xq = din("w_xq", [L, D, DX])
    w_xk = din("w_xk", [L, D, DX])
    w_xv = din("w_xv", [L, D, DX])
    w_xo = din("w_xo", [L, DX, D])
    w_up = din("w_up", [L, D, 2 * DFF])
    f_conv_w = din("f_conv_w", [L, 3, 2 * DFF])
    w_down = din("w_down", [L, DFF, D])

    o_y = dout("o_y", [OWN, D])
    o_ys = dout("o_ys", [SB, D])
    o_mk = dout("o_mk", [L, NMEM, DX])
    o_mv = dout("o_mv", [L, NMEM, DX])
    o_cb = dout("o_cb", [L, 30, DG])
    o_pl = dout("o_pl", [L, 15, DG])
    o_sc = dout("o_sc", [L, 2, DG])
    o_ff = dout("o_ff", [L, 2, 2 * DFF])
    o_cbs = dout("o_cbs", [L, NB, 30, DG])
    o_pls = dout("o_pls", [L, NB, 15, DG])
    o_scs = dout("o_scs", [L, NB, 2, DG])
    o_ffs = dout("o_ffs", [L, NB, 2, 2 * DFF])
    o_vs = dout("o_vs", [L, NB, 4, DG])
    o_dbg = dout("o_dbg", dbg_shape) if dbg_shape else None

    Hh = Region(nc, es, "Hh", KC * TMAX * 4, TMAX * 4)
    XN = Region(nc, es, "XN", KC * TMAX * 2, TMAX * 2)
    RR = Region(nc, es, "RR", KC * TMAX * 4, TMAX * 4)
    FF = Region(nc, es, "FF", 32768, 512)
    NN = Region(nc, es, "NN", 7168, 128)
    CC = Region(nc, es, "CC", 7168, 7168)
    WW = [Region(nc, es, "WW%d" % i, 8192, 4096) for i in range(3)]
    PS = Region(nc, es, "PS", 16384, 2048, psum=True)

    h = Hh.view(0, F32, [KC, TMAX])
    xn = XN.view(0, BF16, [KC, TMAX])

    def hk(kc):
        return Hh.cells[kc]

    def xk(kc):
        return XN.cells[kc]

    CO = {}
    coff = [0]

    def calloc(name, n):
        CO[name] = coff[0]
        coff[0] += n
    for nm in gn:
        calloc(nm, L * KC)
    for nm in ("b_conv_b", "b_gn_g", "b_gn_b", "c_scale"):
        calloc(nm, L * 4)
    calloc("b_conv_w", L * 31 * 4)
    calloc("d_conv_w", L * 3 * 4)
    calloc("identf", 128)
    calloc("tri", 128)
    calloc("identb", 64)
    calloc("onesb", 64)
    calloc("onesf", 128)
    calloc("eps", 1)
    calloc("zero", 1)
    assert coff[0] * 4 <= 7168, coff[0]
    cC = CC.cells[0]

    def cst(name, n, off=0, dt=F32):
        w0 = CO[name] + off
        if dt == BF16:
            return CC.t[:, CO[name]:CO[name] + (n + 1) // 2].bitcast(BF16)[:, off:off + n]
        return CC.t[:, w0:w0 + n]

    identf = cst("identf", 128)
    identb = cst("identb", 128, dt=BF16)
    onesb = cst("onesb", 128, dt=BF16)
    onesf = cst("onesf", 128)
    tri_c = cst("tri", 128)
    eps_c = cst("eps", 1)

    carryB_l = [NN.view(l_ * 1440, F32, [4, 30]) for l_ in range(L)]
    carryC_l = [NN.view(l_ * 1440 + 480, F32, [4, 15]) for l_ in range(L)]
    carryD_l = [NN.view(l_ * 1440 + 720, F32, [4, 2]) for l_ in range(L)]
    carryF_l = [NN.view(l_ * 1440 + 752, F32, [2 * NFF, 2]) for l_ in range(L)]
    smalls = NN.view(5760, F32, [64])
    fcw = NN.view(6016, F32, [3, 2 * NFF])
    FCW0 = 6016 // 4

    ps_rr = [0]
    ps_reserved = set()

    def pst(i):
        return PS.view(i * 4096, F32, [1024])

    def next_pt():
        while True:
            i = ps_rr[0]
            ps_rr[0] = (i + 1) % 4
            if i not in ps_reserved:
                return pst(i)

    def next_pt_idx():
        while True:
            i = ps_rr[0]
            ps_rr[0] = (i + 1) % 4
            if i not in ps_reserved:
                return i

    def bank(v, b):
        return View(v.ap[:, b * 512:(b + 1) * 512], v.cells[b:b + 1])

    def bank_bf(v, b):
        return View(v.ap[:, b * 512:(b + 1) * 512].bitcast(BF16), v.cells[b:b + 1])

    wplan = []
    wstate = dict(issued=0, used=0)

    def wget(parts):
        if T.plan:
            wplan.append(parts)
            return View(WW[0].t[:, :].bitcast(BF16), WW[0].cells)
        i = wstate["used"]
        wstate["used"] += 1
        while wstate["issued"] < min(len(wplan), i + 2):
            k = wstate["issued"]
            reg = WW[k % 3]
            for (eoff, dap, shp) in wplan[k]:
                n = int(np.prod(shp))
                dst = reg.t[:, :].bitcast(BF16)[:, eoff:eoff + n]
                if len(shp) == 2:
                    dst = dst.rearrange("p (a b) -> p a b", a=shp[0])
                c_lo_ = (eoff * 2) // 4096
                c_hi_ = (eoff * 2 + n * 2 - 1) // 4096
                T.dma("g", dst, dap, [], reg.cells[c_lo_:c_hi_ + 1])
            wstate["issued"] += 1
        reg = WW[i % 3]
        return View(reg.t[:, :].bitcast(BF16), reg.cells)

    def w_cols(w, l, c0, n):
        return w[l].rearrange("(kc p) n -> p kc n", p=128)[:, :, c0:c0 + n]

    def P(method, R, W, *a, **k):
        return T.op("p", method, R, W, *a, **k)

    def A(method, R, W, *a, **k):
        return T.op("a", method, R, W, *a, **k)

    def V(method, R, W, *a, **k):
        return T.op("v", method, R, W, *a, **k)

    def G(method, R, W, *a, **k):
        return T.op("g", method, R, W, *a, **k)

    def acopy(R, W, out, in_, scale=None):
        if scale is None:
            return A("activation", R, W, out=out, in_=in_, func=AF.Copy)
        return A("activation", R, W, out=out, in_=in_, func=AF.Copy, scale=scale)

    dbg_state = dict(done=False)
    CTX = {}

    def dump(name, view_ap, cells):
        if stop == name and not T.plan:
            T.dma("g", o_dbg, view_ap, cells, [])
            raise StopEmit()
        if stop == name and T.plan:
            raise StopEmit()

    def load_T(dram2d, nrows, cname, coff0):
        r0 = 0
        while r0 < nrows:
            n = min(128, nrows - r0)
            st = FF.view(0 if (r0 // 128) % 2 == 0 else 512, F32, [128], parts=128)
            T.dma("s", st.ap[0:n, :], dram2d[r0:r0 + n, :], [], [st])
            pt = next_pt()
            P("transpose", [st, cC], [pt], out=pt.ap[:, 0:n], in_=st.ap[0:n, :], identity=identf[0:n, 0:n])
            V("tensor_copy", [pt], [cC], out=cst(cname, n, coff0 + r0), in_=pt.ap[:, 0:n])
            r0 += n

    def emit_consts():
        T.dma("s", identf, ident[:, :], [], [cC])
        T.dma("s", tri_c, tri[:, :], [], [cC])
        V("tensor_copy", [cC], [cC], out=identb, in_=identf)
        V("memset", [], [cC], onesb, 1.0 / D)
        V("memset", [], [cC], onesf, 1.0 / 128)
        V("memset", [], [cC], eps_c, EPS)
        V("memset", [], [cC], cst("zero", 1), 0.0)
        for nm in gn:
            load_T(gn[nm].rearrange("l (k p) -> (l k) p", p=128), L * KC, nm, 0)
        for nm, t in (("b_conv_b", b_conv_b), ("b_gn_g", b_gn_g), ("b_gn_b", b_gn_b), ("c_scale", c_scale)):
            load_T(t.rearrange("l (j p) -> (l j) p", p=128), L * 4, nm, 0)
        load_T(b_conv_w.rearrange("l k (j p) -> (l k j) p", p=128), L * 31 * 4, "b_conv_w", 0)
        load_T(d_conv_w.rearrange("l k (j p) -> (l k j) p", p=128), L * 3 * 4, "d_conv_w", 0)
        V("memset", [], [NN.cells], NN.t[:, 0:1440], 0.0)

    def gcol(nm, l, kc):
        return cst(nm, 1, l * KC + kc)

    def emit_block(blk):
        X = (blk == 0)
        S = SB if X else 0
        TT = TB + S
        NT = [(0, 512), (512, TT - 512)]
        NT3 = [(0, 512), (512, 256)] + ([(768, 64)] if X else [])
        r0 = blk * TB

        def load_tokens(src_rows, ntok, col0, si):
            st = FF.view(si * 8192, F32, [D])
            T.dma("s", st.ap[0:ntok, :], src_rows, [], [st])
            for q in range(4):
                pt = next_pt()
                for i in range(4):
                    kc = q * 4 + i
                    P("transpose", [st, cC], [pt], out=pt.ap[:, i * 128:i * 128 + ntok],
                      in_=st.ap[0:ntok, kc * 128:(kc + 1) * 128], identity=identf[0:ntok, 0:ntok], inc=(i == 3))
                src = pt.ap[:, 0:512].rearrange("p (a b) -> p a b", a=4)[:, :, 0:ntok]
                dst = h.ap[:, q * 4:(q + 1) * 4, col0:col0 + ntok]
                cells = [hk(q * 4 + i) for i in range(4)]
                if q % 2 == 0:
                    acopy([pt], cells, dst, src)
                else:
                    V("tensor_copy", [pt], cells, out=dst, in_=src)
        for tt in range(TB // 128):
            load_tokens(xreg[r0 + tt * 128:r0 + (tt + 1) * 128, :], 128, tt * 128, tt % 2)
        if X:
            load_tokens(xsmp[:, :], SB, TB, 0)
        dump("load%d" % blk, h.ap[:, :, :], Hh.cells)

        sqb = [FF.view(25600, BF16, [TMAX]), FF.view(27264, BF16, [TMAX])]
        rstd = FF.view(28928, F32, [TMAX])
        tmpf = [FF.view(11264, F32, [TMAX]), FF.view(14848, F32, [TMAX])]

        def ms_accum(pms, src_ap, src_cells, idx, n_total):
            sq = sqb[idx % 2]
            A("activation", src_cells, [sq], out=sq.ap[:, 0:TT], in_=src_ap, func=AF.Square)
            for ni, (c0, n) in enumerate(NT):
                P("matmul", [sq, cC], [pms], pms.ap[:, c0:c0 + n], lhsT=onesb, rhs=sq.ap[:, c0:c0 + n],
                  start=(idx == 0), stop=(idx == n_total - 1), inc=(ni == len(NT) - 1))

        def finish_rstd(pms, masked):
            A("activation", [pms, cC], [rstd], out=rstd.ap[:, 0:TT], in_=pms.ap[:, 0:TT], func=AF.Sqrt,
              bias=eps_c, scale=1.0)
            V("reciprocal", [rstd], [rstd], out=rstd.ap[:, 0:TT], in_=rstd.ap[:, 0:TT])
            if masked and X:
                V("tensor_tensor", [rstd, maskc], [rstd], out=rstd.ap[:, 0:HALO], in0=rstd.ap[:, 0:HALO],
                  in1=maskc_ap, op=ALU.mult)

        def prenorm(gname, l, masked):
            i = next_pt_idx()
            ps_reserved.add(i)
            pms = pst(i)
            for kc in range(KC):
                ms_accum(pms, h.ap[:, kc, 0:TT], [hk(kc)], kc, KC)
            finish_rstd(pms, masked)
            ps_reserved.discard(i)
            for kc in range(KC):
                V("scalar_tensor_tensor", [hk(kc), rstd, cC], [xk(kc)], out=xn.ap[:, kc, 0:TT], in0=h.ap[:, kc, 0:TT],
                  scalar=gcol(gname, l, kc), in1=rstd.ap[:, 0:TT], op0=ALU.mult, op1=ALU.mult)

        def postnorm_add(gname, l, pms, src_fn):
            finish_rstd(pms, False)
            for m in range(KC):
                sap, scells = src_fn(m)
                tm = tmpf[m % 2]
                V("scalar_tensor_tensor", scells + [rstd, cC], [tm], out=tm.ap[:, 0:TT], in0=sap,
                  scalar=gcol(gname, l, m), in1=rstd.ap[:, 0:TT], op0=ALU.mult, op1=ALU.mult)
                V("tensor_tensor", [tm, hk(m)], [hk(m)], out=h.ap[:, m, 0:TT], in0=h.ap[:, m, 0:TT],
                  in1=tm.ap[:, 0:TT], op=ALU.add)

        def proj(wt, coff, rhs_view=None, rcells=None, nk=KC, kstride=None):
            pt = next_pt()
            ks = kstride if kstride is not None else 256
            for kc in range(nk):
                for ni, (c0, n) in enumerate(NT):
                    P("matmul", [wt, xk(kc)] if rhs_view is None else [wt] + rcells, [pt], pt.ap[:, c0:c0 + n],
                      lhsT=wt.ap[:, kc * ks + coff:kc * ks + coff + 128],
                      rhs=(xn.ap[:, kc, c0:c0 + n] if rhs_view is None else rhs_view(kc, c0, n)),
                      start=(kc == 0), stop=(kc == nk - 1), inc=(kc == nk - 1 and ni == len(NT) - 1))
            return pt

        def s4(ap, c0=TB):
            return ap[:, c0:c0 + SB].rearrange("p (b i) -> p b i", i=4)

        for l in range(depth):
            CL = 128 * l if X else 0
            NT[:] = [(CL, 512 - CL), (512, TT - 512)]
            NT3[:] = [(CL, 512 - CL), (512, 256)] + ([(768, 64)] if X else [])
            CTX['TT0'] = CL // 128
            CTX['NT3'] = NT3
            emit_layer(blk, l, X, S, TT, NT, prenorm, postnorm_add, proj, s4, ms_accum, sqb, rstd, tmpf)

        def store_tokens(dst_rows, ntok, col0, si):
            st = FF.view(si * 8192, F32, [D])
            for q in range(4):
                pt = next_pt()
                for i in range(4):
                    kc = q * 4 + i
                    P("transpose", [hk(kc), cC], [pt], out=pt.ap[0:ntok, i * 128:(i + 1) * 128],
                      in_=h.ap[:, kc, col0:col0 + ntok], identity=identf, inc=(i == 3))
                if q % 2 == 0:
                    acopy([pt], [st], st.ap[0:ntok, q * 512:(q + 1) * 512], pt.ap[0:ntok, 0:512])
                else:
                    V("tensor_copy", [pt], [st], out=st.ap[0:ntok, q * 512:(q + 1) * 512], in_=pt.ap[0:ntok, 0:512])
            T.dma("s", dst_rows, st.ap[0:ntok, :], [st], [])
        if X:
            for tt in range(4, 6):
                store_tokens(o_y[(tt - 4) * 128:(tt - 3) * 128, :], 128, tt * 128, tt % 2)
            store_tokens(o_ys[:, :], SB, TB, 0)
        else:
            for tt in range(6):
                store_tokens(o_y[256 + tt * 128:256 + (tt + 1) * 128, :], 128, tt * 128, tt % 2)

    MK = Region(nc, es, "MK", HALO * 4, HALO * 4)
    maskc = MK.cells[0]
    maskc_ap = MK.t[:, 0:HALO]
    IV = Region(nc, es, "IV", 64 * 4, 256)
    invc_v = IV.view(0, F32, [4, 16])

    def emit_layer(blk, l, X, S, TT, NT, prenorm, postnorm_add, proj, s4, ms_accum, sqb, rstd, tmpf):
        tag = "b%dl%d" % (blk, l)
        carryB, carryC, carryD, carryF = carryB_l[l], carryC_l[l], carryD_l[l], carryF_l[l]

        def cw(nm, per, idx):
            return cst(nm, 1, l * per + idx)

        prenorm("g_mix_pre", l, True)
        dump("xn_" + tag, xn.ap[:, :, :], XN.cells)

        y = RR.view(0, BF16, [KC, TMAX])

        def yc(i):
            return RR.cells[i // 2]
        SLOT = [RR.view(26624 + i * 6656, F32, [1664]) for i in range(4)]

        prevB = FF.view(7168, F32, [4, NB * 30])
        prevC = FF.view(14848, F32, [4, NB * 15])
        prevD = FF.view(18688, F32, [4, NB * 2])
        if X:
            def load_prev(src, rows_per_b, dstv):
                nrows = NB * rows_per_b
                ntile = (nrows + 127) // 128
                rpt = (nrows + ntile - 1) // ntile
                flat = src[l].rearrange("b r c -> (b r) c")
                pts = [next_pt() for _ in range(2)]
                for ti in range(ntile):
                    a0 = ti * rpt
                    n = min(rpt, nrows - a0)
                    st = FF.view(19200 + (ti % 2) * 2048, F32, [DG])
                    T.dma("s", st.ap[0:n, :], flat[a0:a0 + n, :], [], [st])
                    for j in range(4):
                        pb = bank(pts[j // 2], j % 2)
                        P("transpose", [st, cC], [pb], out=pb.ap[:, a0:a0 + n], in_=st.ap[0:n, j * 128:(j + 1) * 128],
                          identity=identf[0:n, 0:n], inc=(j == 3))
                for j in range(4):
                    pb = bank(pts[j // 2], j % 2)
                    acopy([pb], [dstv], dstv.ap[:, j, 0:nrows], pb.ap[:, 0:nrows])
            load_prev(scb, 30, prevB)
            load_prev(spl, 15, prevC)
            load_prev(ssc, 2, prevD)
            T.dma("s", o_cbs[l, :, 0:26, :], scb[l, :, 4:30, :], [], [])
            T.dma("s", o_pls[l, :, 0:11, :], spl[l, :, 4:15, :], [], [])


        for j in range(4):
            wA = wget([(0, w_cols(w_in, l, 2560 + 128 * j, 128), [KC, 128]), (2048, w_cols(w_in, l, 3072 + 128 * j, 128), [KC, 128])])
            wB = wget([(0, w_cols(w_in, l, 3584 + 128 * j, 128), [KC, 128])])
            px = proj(wA, 0, kstride=128)
            pb_ = proj(wA, 2048, kstride=128)
            pc = proj(wB, 0, kstride=128)
            xs, bs, ext, acc = SLOT
            acopy([px], [xs], xs.ap[:, 0:TT], px.ap[:, 0:TT])
            acopy([pb_], [bs], bs.ap[:, 0:TT], pb_.ap[:, 0:TT])
            if j == 0:
                dump("xs_" + tag, xs.ap[:, 0:TMAX], [xs])
                dump("bs_" + tag, bs.ap[:, 0:TMAX], [bs])
            V("tensor_tensor", [pc, xs], [ext], out=ext.ap[:, 2:2 + TB], in0=pc.ap[:, 0:TB], in1=xs.ap[:, 0:TB], op=ALU.mult)
            if blk == 0:
                V("memset", [], [ext], ext.ap[:, 0:2], 0.0)
            else:
                V("tensor_copy", [carryD], [ext], out=ext.ap[:, 0:2], in_=carryD.ap[:, j, :])
            E = 2 + TB
            if X:
                exs = ext.ap[:, E:E + 96].rearrange("p (b s) -> p b s", s=6)
                V("tensor_tensor", [pc, xs], [ext], out=exs[:, :, 2:6], in0=s4(pc.ap), in1=s4(xs.ap), op=ALU.mult)
                V("tensor_copy", [prevD], [ext], out=exs[:, :, 0:2], in_=prevD.ap[:, j, :].rearrange("p (b r) -> p b r", r=2))
                E += 96
            V("tensor_copy", [ext], [carryD], out=carryD.ap[:, j, :], in_=ext.ap[:, TB:TB + 2])
            LN = E - 2
            dw = lambda k: cst("d_conv_w", 1, (l * 3 + k) * 4 + j)
            V("tensor_scalar", [ext, cC], [acc], out=acc.ap[:, 0:LN], in0=ext.ap[:, 0:LN], scalar1=dw(0), scalar2=None, op0=ALU.mult)
            V("scalar_tensor_tensor", [ext, acc, cC], [acc], out=acc.ap[:, 0:LN], in0=ext.ap[:, 1:1 + LN], scalar=dw(1), in1=acc.ap[:, 0:LN], op0=ALU.mult, op1=ALU.add)
            V("scalar_tensor_tensor", [ext, acc, cC], [acc], out=acc.ap[:, 0:LN], in0=ext.ap[:, 2:2 + LN], scalar=dw(2), in1=acc.ap[:, 0:LN], op0=ALU.mult, op1=ALU.add)
            if j == 0:
                dump("ext_" + tag, ext.ap[:, 0:TMAX + 64], [ext])
                dump("acc_" + tag, acc.ap[:, 0:TMAX + 64], [acc])
            V("tensor_tensor", [acc, bs], [yc(12 + j)], out=y.ap[:, 12 + j, 0:TB], in0=acc.ap[:, 0:TB], in1=bs.ap[:, 0:TB], op=ALU.mult)
            dump("yd%d_" % j + tag, y.ap[:, 12:16, :], RR.cells)
            if X:
                accs = acc.ap[:, TB + 2:TB + 2 + 96].rearrange("p (b s) -> p b s", s=6)[:, :, 0:4]
                V("tensor_tensor", [acc, bs], [yc(12 + j)], out=s4(y.ap[:, 12 + j, :]), in0=accs, in1=s4(bs.ap), op=ALU.mult)
                emit_rows_sample(l, "D", j, ext.ap[:, 2 + TB:2 + TB + 96].rearrange("p (b s) -> p b s", s=6)[:, :, 4:6], [ext], 2)
        if blk == 1:
            emit_rows_prompt(l, o_sc, carryD, 2)
        dump("yd_" + tag, y.ap[:, 12:16, :], RR.cells)

        clin = FF.view(24320, BF16, [4, 128])
        T.dma("g", clin.ap, c_lin[l].rearrange("g c d -> c g d"), [], [clin])
        for g in range(4):
            if g % 2 == 0:
                wA = wget([(0, w_cols(w_in, l, 2048 + 128 * g, 256), [KC, 256])])
            px = proj(wA, (g % 2) * 128)
            ext, s1, s2, pool_s = SLOT
            win = 2 << g
            acopy([px], [ext], ext.ap[:, 15:15 + TB], px.ap[:, 0:TB])
            if blk == 0:
                V("memset", [], [ext], ext.ap[:, 0:15], 0.0)
            else:
                V("tensor_copy", [carryC], [ext], out=ext.ap[:, 0:15], in_=carryC.ap[:, g, :])
            E = 15 + TB
            if X:
                exs = ext.ap[:, E:E + NB * 19].rearrange("p (b s) -> p b s", s=19)
                acopy([px], [ext], exs[:, :, 15:19], s4(px.ap))
                V("tensor_copy", [prevC], [ext], out=exs[:, :, 0:15], in_=prevC.ap[:, g, :].rearrange("p (b r) -> p b r", r=15))
                E += NB * 19
            V("tensor_copy", [ext], [carryC], out=carryC.ap[:, g, :], in_=ext.ap[:, TB:TB + 15])
            cur = ext
            bufs = [s1, s2]
            for st_ in range(g + 1):
                sh = 1 << st_
                nxt = bufs[st_ % 2]
                V("tensor_tensor", [cur], [nxt], out=nxt.ap[:, sh:E], in0=cur.ap[:, sh:E], in1=cur.ap[:, 0:E - sh], op=ALU.add)
                if st_ == 0:
                    V("tensor_copy", [cur], [nxt], out=nxt.ap[:, 0:1], in_=cur.ap[:, 0:1])
                else:
                    V("tensor_copy", [cur], [nxt], out=nxt.ap[:, 0:sh], in_=cur.ap[:, 0:sh])
                cur = nxt
            pooled = View(pool_s.ap.bitcast(BF16), pool_s.cells)
            V("scalar_tensor_tensor", [cur, ext], [pooled], out=pooled.ap[:, 15:E], in0=cur.ap[:, 15:E], scalar=1.0 / win,
              in1=ext.ap[:, 15:E], op0=ALU.mult, op1=ALU.subtract)
            if X:
                tq = FF.view(25344, F32, [16])
                V("tensor_tensor", [cur, IV.cells[0]], [tq], out=tq.ap, in0=cur.ap[:, 15 + HALO:15 + HALO + 16], in1=invc_v.ap[:, g, :], op=ALU.mult)
                V("tensor_tensor", [tq, ext], [pooled], out=pooled.ap[:, 15 + HALO:15 + HALO + 16], in0=tq.ap,
                  in1=ext.ap[:, 15 + HALO:15 + HALO + 16], op=ALU.subtract)
            pt = next_pt()
            NT3 = CTX['NT3']
            for ni, (c0, n) in enumerate(NT3):
                if c0 < TB:
                    rhs = pooled.ap[:, 15 + c0:15 + c0 + n]
                else:
                    rhs = pooled.ap[:, 15 + TB:15 + TB + NB * 19].rearrange("p (b s) -> p b s", s=19)[:, :, 15:19]
                P("matmul", [clin, pooled], [pt], pt.ap[:, c0:c0 + n], lhsT=clin.ap[:, g, :], rhs=rhs, start=True, stop=True,
                  inc=(ni == len(NT3) - 1))
            acopy([pt, cC], [yc(8 + g)], y.ap[:, 8 + g, 0:TT], pt.ap[:, 0:TT], scale=cw("c_scale", 4, g))
            if X:
                emit_rows_sample(l, "C", g, ext.ap[:, 15 + TB:15 + TB + NB * 19].rearrange("p (b s) -> p b s", s=19)[:, :, 15:19], [ext], 4)
        if blk == 1:
            emit_rows_prompt(l, o_pl, carryC, 15)
        dump("yc_" + tag, y.ap[:, 8:12, :], RR.cells)

        for j in range(4):
            wA = wget([(0, w_cols(w_in, l, 1024 + 128 * j, 128), [KC, 128]), (2048, w_cols(w_in, l, 1536 + 128 * j, 128), [KC, 128])])
            pa = proj(wA, 0, kstride=128)
            pg = proj(wA, 2048, kstride=128)
            sg, ext, acc, s3 = SLOT
            A("activation", [pg], [sg], out=sg.ap[:, 0:TT], in_=pg.ap[:, 0:TT], func=AF.Sigmoid)
            V("tensor_tensor", [pa, sg], [ext], out=ext.ap[:, 30:30 + TB], in0=pa.ap[:, 0:TB], in1=sg.ap[:, 0:TB], op=ALU.mult)
            if blk == 0:
                V("memset", [], [ext], ext.ap[:, 0:30], 0.0)
            else:
                V("tensor_copy", [carryB], [ext], out=ext.ap[:, 0:30], in_=carryB.ap[:, j, :])
            E = 30 + TB
            if X:
                exs = ext.ap[:, E:E + NB * 34].rearrange("p (b s) -> p b s", s=34)
                V("tensor_tensor", [pa, sg], [ext], out=exs[:, :, 30:34], in0=s4(pa.ap), in1=s4(sg.ap), op=ALU.mult)
                V("tensor_copy", [prevB], [ext], out=exs[:, :, 0:30], in_=prevB.ap[:, j, :].rearrange("p (b r) -> p b r", r=30))
                E += NB * 34
            V("tensor_copy", [ext], [carryB], out=carryB.ap[:, j, :], in_=ext.ap[:, TB:TB + 30])
            LN = E - 30
            bw = lambda k: cst("b_conv_w", 1, (l * 31 + k) * 4 + j)
            V("tensor_scalar", [ext, cC], [acc], out=acc.ap[:, 0:LN], in0=ext.ap[:, 0:LN], scalar1=bw(0), scalar2=cw("b_conv_b", 4, j),
              op0=ALU.mult, op1=ALU.add)
            for k in range(1, 31):
                V("scalar_tensor_tensor", [ext, acc, cC], [acc], out=acc.ap[:, 0:LN], in0=ext.ap[:, k:k + LN], scalar=bw(k),
                  in1=acc.ap[:, 0:LN], op0=ALU.mult, op1=ALU.add)
            if X:
                emit_rows_sample(l, "B", j, ext.ap[:, 30 + TB:30 + TB + NB * 34].rearrange("p (b s) -> p b s", s=34)[:, :, 30:34], [ext], 4)
            yb = ext
            acopy([acc], [yb], yb.ap[:, 0:TB], acc.ap[:, 0:TB])
            if X:
                acopy([acc], [yb], s4(yb.ap), acc.ap[:, TB + 30:TB + 30 + NB * 34].rearrange("p (b s) -> p b s", s=34)[:, :, 0:4])
            ysq = sg
            A("activation", [yb], [ysq], out=ysq.ap[:, 0:TT], in_=yb.ap[:, 0:TT], func=AF.Square)
            pm = next_pt()
            pq = next_pt()
            for ni, (c0, n) in enumerate(NT):
                P("matmul", [yb, cC], [pm], pm.ap[:, c0:c0 + n], lhsT=onesf, rhs=yb.ap[:, c0:c0 + n], start=True, stop=True, inc=False)
                P("matmul", [ysq, cC], [pq], pq.ap[:, c0:c0 + n], lhsT=onesf, rhs=ysq.ap[:, c0:c0 + n], start=True, stop=True,
                  inc=(ni == len(NT) - 1))
            m2 = acc
            A("activation", [pm], [m2], out=m2.ap[:, 0:TT], in_=pm.ap[:, 0:TT], func=AF.Square)
            V("tensor_tensor", [pq, m2], [m2], out=m2.ap[:, 0:TT], in0=pq.ap[:, 0:TT], in1=m2.ap[:, 0:TT], op=ALU.subtract)
            V("tensor_scalar", [m2], [m2], out=m2.ap[:, 0:TT], in0=m2.ap[:, 0:TT], scalar1=0.0, scalar2=None, op0=ALU.max)
            A("activation", [m2, cC], [m2], out=m2.ap[:, 0:TT], in_=m2.ap[:, 0:TT], func=AF.Sqrt, bias=eps_c, scale=1.0)
            V("reciprocal", [m2], [m2], out=m2.ap[:, 0:TT], in_=m2.ap[:, 0:TT])
            dd = s3
            V("tensor_tensor", [yb, pm], [dd], out=dd.ap[:, 0:TT], in0=yb.ap[:, 0:TT], in1=pm.ap[:, 0:TT], op=ALU.subtract)
            V("tensor_tensor", [dd, m2], [dd], out=dd.ap[:, 0:TT], in0=dd.ap[:, 0:TT], in1=m2.ap[:, 0:TT], op=ALU.mult)
            A("activation", [dd, cC], [yc(4 + j)], out=y.ap[:, 4 + j, 0:TT], in_=dd.ap[:, 0:TT], func=AF.Silu,
              bias=cw("b_gn_b", 4, j), scale=cw("b_gn_g", 4, j))
        if blk == 1:
            emit_rows_prompt(l, o_cb, carryB, 30)
        dump("yb_" + tag, y.ap[:, 4:8, :], RR.cells)

        emit_mixer_a(blk, l, X, S, TT, NT, proj, s4, y, yc, SLOT)
        dump("ya_" + tag, y.ap[:, 0:4, :], RR.cells)

        mixs = RR.view(26624, BF16, [KC, TMAX])
        ip = next_pt_idx()
        ps_reserved.add(ip)
        pms = pst(ip)
        for m in range(KC):
            if m % 2 == 0:
                wt = wget([(0, w_cols(w_out, l, m * 128, 256), [KC, 256])])
            pt = next_pt()
            for kc in range(KC):
                for ni, (c0, n) in enumerate(NT):
                    P("matmul", [wt, yc(kc)], [pt], pt.ap[:, c0:c0 + n], lhsT=wt.ap[:, kc * 256 + (m % 2) * 128:kc * 256 + (m % 2) * 128 + 128],
                      rhs=y.ap[:, kc, c0:c0 + n], start=(kc == 0), stop=(kc == KC - 1), inc=(kc == KC - 1 and ni == len(NT) - 1))
            mc = [RR.cells[8 + m // 2]]
            acopy([pt], mc, mixs.ap[:, m, 0:TT], pt.ap[:, 0:TT])
            ms_accum(pms, pt.ap[:, 0:TT], [pt], m, KC)
        ps_reserved.discard(ip)
        postnorm_add("g_mix_post", l, pms, lambda m: (mixs.ap[:, m, 0:TT], [RR.cells[8 + m // 2]]))
        dump("hmix_" + tag, h.ap[:, :, :], Hh.cells)

        emit_attention(blk, l, X, S, TT, NT, prenorm, postnorm_add, proj, ms_accum)
        dump("hatt_" + tag, h.ap[:, :, :], Hh.cells)

        emit_ffn(blk, l, X, S, TT, NT, prenorm, postnorm_add, proj, s4, ms_accum)
        dump("hffn_" + tag, h.ap[:, :, :], Hh.cells)

    STS = Region(nc, es, "STS", 4 * 64 * 4, 4 * 64 * 4)
    STO = Region(nc, es, "STO", DG * 4, DG * 4)

    def emit_rows_sample(l, which, j, src_ap, src_cells, nr):
        sts = STS.view(0, F32, [4, 64])
        n = NB * nr
        V("tensor_copy", src_cells, [sts], out=sts.ap[:, j, 0:n].rearrange("p (r b) -> p b r", b=NB), in_=src_ap)
        if j < 3:
            return
        pt = next_pt()
        for jj in range(4):
            P("transpose", [sts, cC], [pt], out=pt.ap[0:n, jj * 128:(jj + 1) * 128], in_=sts.ap[:, jj, 0:n], identity=identf,
              inc=(jj == 3))
        sto = STO.view(0, F32, [DG])
        acopy([pt], [sto], sto.ap[0:n, :], pt.ap[0:n, 0:512])
        for r in range(nr):
            if which == "B":
                dst = o_cbs[l, :, 26 + r, :]
            elif which == "C":
                dst = o_pls[l, :, 11 + r, :]
            else:
                dst = o_scs[l, :, r, :]
            T.dma("s", dst, sto.ap[r * NB:(r + 1) * NB, :], [sto], [])

    def emit_rows_prompt(l, dst, carry, nr):
        pt = next_pt()
        for jj in range(4):
            P("transpose", [carry, cC], [pt], out=pt.ap[0:nr, jj * 128:(jj + 1) * 128], in_=carry.ap[:, jj, :], identity=identf,
              inc=(jj == 3))
        sto = STO.view(0, F32, [DG])
        acopy([pt], [sto], sto.ap[0:nr, :], pt.ap[0:nr, 0:512])
        T.dma("s", dst[l], sto.ap[0:nr, :], [sto], [])

    def emit_mixer_a(blk, l, X, S, TT, NT, proj, s4, y, yc, SLOT):
        ntt = TB // 128 + (1 if X else 0)
        vn = FF.view(0, BF16, [7, DG])
        gv = FF.view(19200, F32, [DG])
        bias_bc = FF.view(21248, F32, [4, 128])
        WT = FF.view(23296, BF16, [4, 128])
        vns32 = FF.view(26112, F32, [DG])
        junk = RR.view(26624 + 3 * 6656, BF16, [DG])
        t32 = SLOT[2]
        st = View(smalls.ap, smalls.cells)
        T.dma("s", gv.ap, a_norm_g[l].partition_broadcast(128), [], [gv])
        T.dma("s", bias_bc.ap, a_bs[l].partition_broadcast(128), [], [bias_bc])
        for hd in range(4):
            ws = SLOT[0]
            T.dma("s", ws.ap[:, 0:128], a_ws[l, hd], [], [ws])
            pt = next_pt()
            P("transpose", [ws, cC], [pt], out=pt.ap[:, 0:128], in_=ws.ap[:, 0:128], identity=identf)
            V("tensor_tensor", [pt, cC], [WT], out=WT.ap[:, hd, :], in0=pt.ap[:, 0:128], in1=tri_c, op=ALU.mult)
        if X:
            wt4 = FF.view(25408, F32, [4, 16])
            for hd in range(4):
                T.dma("s", wt4.ap[:, hd, :].rearrange("p (i j) -> p i j", j=4), a_ws[l, hd, 0:4, 0:4].partition_broadcast(128), [], [wt4])
        wv0 = wget([(0, w_cols(w_in, l, 512, 256), [KC, 256])])
        wv1 = wget([(0, w_cols(w_in, l, 768, 256), [KC, 256])])
        for tt in range(CTX['TT0'], ntt):
            ntok = 128 if tt < TB // 128 else SB
            c0 = tt * 128
            ptv = next_pt()
            pv = bank(ptv, 0)
            for hf, wv in enumerate((wv0, wv1)):
                for kc in range(KC):
                    P("matmul", [wv, xk(kc)], [pv], pv.ap[0:ntok, hf * 256:(hf + 1) * 256], lhsT=xn.ap[:, kc, c0:c0 + ntok],
                      rhs=wv.ap[:, kc * 256:(kc + 1) * 256], start=(kc == 0), stop=(kc == KC - 1), inc=(hf == 1 and kc == KC - 1))
            V("memset", [], [st], st.ap[:, 0:2], 0.0)
            A("activation", [pv], [junk, st], out=junk.ap[0:ntok, :], in_=pv.ap[0:ntok, :], func=AF.Copy, accum_out=st.ap[0:ntok, 0:1])
            A("activation", [pv], [junk, st], out=junk.ap[0:ntok, :], in_=pv.ap[0:ntok, :], func=AF.Square, accum_out=st.ap[0:ntok, 1:2])
            V("tensor_scalar", [st], [st], out=st.ap[:, 2:3], in0=st.ap[:, 0:1], scalar1=1.0 / DG, scalar2=None, op0=ALU.mult)
            V("tensor_tensor", [st], [st], out=st.ap[:, 3:4], in0=st.ap[:, 2:3], in1=st.ap[:, 2:3], op=ALU.mult)
            V("scalar_tensor_tensor", [st], [st], out=st.ap[:, 4:5], in0=st.ap[:, 1:2], scalar=1.0 / DG, in1=st.ap[:, 3:4],
              op0=ALU.mult, op1=ALU.subtract)
            V("tensor_scalar", [st], [st], out=st.ap[:, 4:5], in0=st.ap[:, 4:5], scalar1=0.0, scalar2=None, op0=ALU.max)
            A("activation", [st, cC], [st], out=st.ap[:, 5:6], in_=st.ap[:, 4:5], func=AF.Sqrt, bias=eps_c, scale=1.0)
            V("reciprocal", [st], [st], out=st.ap[:, 5:6], in_=st.ap[:, 5:6])
            V("tensor_scalar", [pv, st], [t32], out=t32.ap[0:ntok, 0:DG], in0=pv.ap[0:ntok, :], scalar1=st.ap[0:ntok, 2:3], scalar2=st.ap[0:ntok, 5:6],
              op0=ALU.subtract, op1=ALU.mult)
            V("tensor_tensor", [t32, gv], [vn], out=vn.ap[0:ntok, tt, :], in0=t32.ap[0:ntok, 0:DG], in1=gv.ap[0:ntok, :], op=ALU.mult)
            if ntok == SB:
                V("tensor_tensor", [t32, gv], [vns32], out=vns32.ap[0:ntok, :], in0=t32.ap[0:ntok, 0:DG], in1=gv.ap[0:ntok, :], op=ALU.mult)
                T.dma("s", o_vs[l].rearrange("b i c -> (b i) c"), vns32.ap[0:SB, :], [vns32], [])
        for hd in range(4):
            if hd % 2 == 0:
                wu = wget([(0, w_cols(w_in, l, 128 * hd, 256), [KC, 256])])
            pu = proj(wu, (hd % 2) * 128)
            pz = next_pt()
            for tt in range(CTX['TT0'], TB // 128):
                P("matmul", [vn, WT], [pz], pz.ap[:, tt * 128:(tt + 1) * 128], lhsT=vn.ap[:, tt, hd * 128:(hd + 1) * 128],
                  rhs=WT.ap[:, hd, :], start=True, stop=True, inc=(tt == TB // 128 - 1))
            us, t1 = SLOT[0], SLOT[1]
            acopy([pu], [us], us.ap[:, 0:TT], pu.ap[:, 0:TT])
            V("tensor_tensor", [pz, bias_bc], [t1], out=t1.ap[:, 0:TB].rearrange("p (a b) -> p a b", b=128),
              in0=pz.ap[:, 0:TB].rearrange("p (a b) -> p a b", b=128), in1=bc_mid(bias_bc.ap[:, hd, :], TB // 128), op=ALU.add)
            V("tensor_tensor", [t1, us], [yc(hd)], out=y.ap[:, hd, 0:TB], in0=t1.ap[:, 0:TB], in1=us.ap[:, 0:TB], op=ALU.mult)
            if X:
                pT = next_pt()
                P("transpose", [vns32, cC], [pT], out=pT.ap[:, 0:SB], in_=vns32.ap[0:SB, hd * 128:(hd + 1) * 128], identity=identf[0:SB, 0:SB])
                vT = View(t1.ap[:, TB:TB + SB], t1.cells)
                acopy([pT], [t1], vT.ap, pT.ap[:, 0:SB])
                vT3 = vT.ap.rearrange("p (b i) -> p b i", i=4)
                za = View(t1.ap[:, TB + SB:TB + 2 * SB], t1.cells)
                za3 = za.ap.rearrange("p (b i) -> p b i", i=4)
                for i in range(4):
                    for jq in range(i + 1):
                        wsc = wt4.ap[:, hd, i * 4 + jq:i * 4 + jq + 1]
                        if jq == 0:
                            V("tensor_scalar", [t1, wt4], [t1], out=za3[:, :, i:i + 1], in0=vT3[:, :, jq:jq + 1], scalar1=wsc, scalar2=None, op0=ALU.mult)
                        else:
                            V("scalar_tensor_tensor", [t1, wt4], [t1], out=za3[:, :, i:i + 1], in0=vT3[:, :, jq:jq + 1], scalar=wsc,
                              in1=za3[:, :, i:i + 1], op0=ALU.mult, op1=ALU.add)
                V("tensor_tensor", [t1, bias_bc], [t1], out=za3, in0=za3, in1=bc_mid(bias_bc.ap[:, hd, 0:4], NB), op=ALU.add)
                V("tensor_tensor", [t1, us], [yc(hd)], out=s4(y.ap[:, hd, :]), in0=za3, in1=s4(us.ap), op=ALU.mult)

    def emit_attention(blk, l, X, S, TT, NT, prenorm, postnorm_add, proj, ms_accum):
        prenorm("g_x_pre", l, False)
        qT = RR.view(0, BF16, [4, TMAX])
        oT = RR.view(6656, BF16, [4, TMAX])
        memT = RR.view(13312, BF16, [KC, NMEM])
        qm = RR.view(21504, BF16, [4, NB, SB])
        Pn = RR.view(29696, BF16, [4, NMEM])
        PT = RR.view(31744, BF16, [8, 128])
        kb = [RR.view(33792 + i * 2048, BF16, [2, DX]) for i in range(2)]
        KTb = [RR.view(37888 + i * 2048, BF16, [4, NMEM]) for i in range(2)]
        vb = [RR.view(41984 + i * 2048, BF16, [2, DX]) for i in range(2)]
        PTs = RR.view(48128, BF16, [8, SB])
        memn = FF.view(16384, BF16, [D])
        KT = FF.view(20480, BF16, [4, NMEM])
        Vt = FF.view(22528, BF16, [2, DX])
        kvo = [FF.view(24576, F32, [DX]), FF.view(11264, F32, [DX])]
        st = View(smalls.ap, smalls.cells)

        for hc in range(4):
            if hc % 2 == 0:
                wq = wget([(0, w_cols(w_xq, l, hc * 128, 256), [KC, 256])])
            pq = proj(wq, (hc % 2) * 128)
            A("activation", [pq], [qT], out=qT.ap[:, hc, 0:TT], in_=pq.ap[:, 0:TT], func=AF.Copy, scale=QSCALE)

        dump("aq_b%dl%d" % (blk, l), qT.ap[:, :, :], RR.cells)
        for mt in range(2):
            ms_ = FF.view(mt * 8192, F32, [D])
            T.dma("s", ms_.ap, memp[mt * 128:(mt + 1) * 128, :], [], [ms_])
            V("memset", [], [st], st.ap[:, 8:9], 0.0)
            A("activation", [ms_], [memn, st], out=memn.ap, in_=ms_.ap, func=AF.Square, accum_out=st.ap[:, 8:9])
            V("tensor_scalar", [st], [st], out=st.ap[:, 9:10], in0=st.ap[:, 8:9], scalar1=1.0 / D, scalar2=None, op0=ALU.mult)
            A("activation", [st, cC], [st], out=st.ap[:, 10:11], in_=st.ap[:, 9:10], func=AF.Sqrt, bias=eps_c, scale=1.0)
            V("reciprocal", [st], [st], out=st.ap[:, 10:11], in_=st.ap[:, 10:11])
            A("activation", [ms_, st], [memn], out=memn.ap, in_=ms_.ap, func=AF.Copy, scale=st.ap[:, 10:11])
            for q in range(2):
                pt = next_pt()
                pbf = bank_bf(pt, 0)
                for i in range(8):
                    kc = q * 8 + i
                    P("transpose", [memn, cC], [pbf], out=pbf.ap[:, i * 128:(i + 1) * 128], in_=memn.ap[:, kc * 128:(kc + 1) * 128],
                      identity=identb, inc=(i == 7))
                for i in range(8):
                    kc = q * 8 + i
                    A("activation", [pbf, cC], [memT], out=memT.ap[:, kc, mt * 128:(mt + 1) * 128], in_=pbf.ap[:, i * 128:(i + 1) * 128],
                      func=AF.Copy, scale=gcol("g_mem", l, kc))
        dump("amem_b%dl%d" % (blk, l), memT.ap[:, :, :], RR.cells)
        wks = []
        for h2 in range(2):
            wk = wget([(0, w_cols(w_xk, l, h2 * 256, 256), [KC, 256])])
            wks.append(wk)
            for hh in range(2):
                hc = h2 * 2 + hh
                pt = next_pt()
                pk = bank(pt, 0)
                for kc in range(KC):
                    P("matmul", [wk, memT], [pk], pk.ap[:, 0:NMEM], lhsT=wk.ap[:, kc * 256 + hh * 128:kc * 256 + hh * 128 + 128],
                      rhs=memT.ap[:, kc, :], start=(kc == 0), stop=(kc == KC - 1), inc=(kc == KC - 1))
                acopy([pk], [KT], KT.ap[:, hc, :], pk.ap[:, 0:NMEM])
            if blk == 0:
                for mt in range(2):
                    pt = next_pt()
                    pk = bank(pt, 0)
                    for kc in range(KC):
                        P("matmul", [wk, memT], [pk], pk.ap[:, 0:256], lhsT=memT.ap[:, kc, mt * 128:(mt + 1) * 128],
                          rhs=wk.ap[:, kc * 256:(kc + 1) * 256], start=(kc == 0), stop=(kc == KC - 1), inc=(kc == KC - 1))
                    ko = kvo[mt]
                    acopy([pk], [ko], ko.ap[:, h2 * 256:(h2 + 1) * 256], pk.ap[:, 0:256])
                    if h2 == 1:
                        T.dma("s", o_mk[l, mt * 128:(mt + 1) * 128, :], ko.ap, [ko], [])
        for h2 in range(2):
            wv = wget([(0, w_cols(w_xv, l, h2 * 256, 256), [KC, 256])])
            for mt in range(2):
                pt = next_pt()
                pk = bank(pt, 0)
                for kc in range(KC):
                    P("matmul", [wv, memT], [pk], pk.ap[:, 0:256], lhsT=memT.ap[:, kc, mt * 128:(mt + 1) * 128],
                      rhs=wv.ap[:, kc * 256:(kc + 1) * 256], start=(kc == 0), stop=(kc == KC - 1), inc=(kc == KC - 1))
                if blk == 0:
                    ko = kvo[mt]
                    acopy([pk], [ko], ko.ap[:, h2 * 256:(h2 + 1) * 256], pk.ap[:, 0:256])
                    V("tensor_copy", [ko], [Vt], out=Vt.ap[:, mt, h2 * 256:(h2 + 1) * 256], in_=ko.ap[:, h2 * 256:(h2 + 1) * 256])
                    if h2 == 1:
                        T.dma("s", o_mv[l, mt * 128:(mt + 1) * 128, :], ko.ap, [ko], [])
                else:
                    acopy([pk], [Vt], Vt.ap[:, mt, h2 * 256:(h2 + 1) * 256], pk.ap[:, 0:256])

        dump("akv_b%dl%d" % (blk, l), KT.ap[:, :, :], [KT, Vt])
        def softmax(psc, npart, dstP):
            sc3 = psc.ap[0:npart, :].rearrange("p (h m) -> p h m", h=4)
            V("tensor_reduce", [psc], [st], out=st.ap[0:npart, 16:20], in_=sc3, axis=AX.X, op=ALU.max)
            V("tensor_scalar", [st], [st], out=st.ap[0:npart, 20:24], in0=st.ap[0:npart, 16:20], scalar1=-1.0, scalar2=None, op0=ALU.mult)
            V("memset", [], [st], st.ap[:, 24:28], 0.0)
            for hh in range(4):
                A("activation", [psc, st], [dstP, st], out=dstP.ap[0:npart, hh, :], in_=sc3[:, hh, :], func=AF.Exp,
                  bias=st.ap[0:npart, 20 + hh:21 + hh], scale=1.0, accum_out=st.ap[0:npart, 24 + hh:25 + hh])
            V("reciprocal", [st], [st], out=st.ap[0:npart, 28:32], in_=st.ap[0:npart, 24:28])
            V("tensor_tensor", [dstP, st], [dstP], out=dstP.ap[0:npart, :, :], in0=dstP.ap[0:npart, :, :],
              in1=bc_last(st.ap[0:npart, 28:32], NMEM), op=ALU.mult)

        for tt in range(CTX['TT0'], TB // 128):
            psc = next_pt()
            for hh in range(4):
                P("matmul", [qT, KT], [psc], psc.ap[:, hh * 256:(hh + 1) * 256], lhsT=qT.ap[:, hh, tt * 128:(tt + 1) * 128],
                  rhs=KT.ap[:, hh, :], start=True, stop=True, inc=(hh == 3))
            softmax(psc, 128, Pn)
            ptp = next_pt()
            pbf = bank_bf(ptp, 0)
            for mc in range(2):
                for hh in range(4):
                    P("transpose", [Pn, cC], [pbf], out=pbf.ap[:, (mc * 4 + hh) * 128:(mc * 4 + hh + 1) * 128],
                      in_=Pn.ap[:, hh, mc * 128:(mc + 1) * 128], identity=identb, inc=(mc == 1 and hh == 3))
            V("tensor_copy", [pbf], [PT], out=PT.ap, in_=pbf.ap.rearrange("p (a b) -> p a b", a=8))
            po_t = next_pt()
            po = bank(po_t, 0)
            for hh in range(4):
                for mc in range(2):
                    P("matmul", [Vt, PT], [po], po.ap[:, hh * 128:(hh + 1) * 128], lhsT=Vt.ap[:, mc, hh * 128:(hh + 1) * 128],
                      rhs=PT.ap[:, mc * 4 + hh, :], start=(mc == 0), stop=(mc == 1), inc=(hh == 3 and mc == 1))
            acopy([po], [oT], oT.ap[:, :, tt * 128:(tt + 1) * 128], po.ap.rearrange("p (a b) -> p a b", a=4))

        dump("aprm_b%dl%d" % (blk, l), oT.ap[:, :, :], RR.cells)
        if X:
            G("memset", [], [qm], qm.ap, 0.0)
            for hh in range(4):
                dst = bass.AP(qm.ap.tensor, qm.ap.offset + hh * NB * SB, [list(qm.ap.ap[0]), [SB + 4, NB], [1, 4]])
                V("tensor_copy", [qT], [qm], out=dst, in_=qT.ap[:, hh, TB:TB + SB].rearrange("p (b i) -> p b i", i=4))
            ipss = next_pt_idx()
            ps_reserved.add(ipss)
            ipss2 = next_pt_idx()
            ps_reserved.add(ipss2)
            pss = pst(ipss)
            pss2 = pst(ipss2)
            pssb = [bank(pss, 0), bank(pss, 1), bank(pss2, 0), bank(pss2, 1)]
            sc_s = RR.view(49152, F32, [4, NMEM], parts=128)
            for b in range(NB):
                kbb = kb[b % 2]
                T.dma("g", kbb.ap, ck[l, b].rearrange("(mc p) d -> p mc d", p=128), [], [kbb])
                ptk = next_pt()
                pbf = bank_bf(ptk, 0)
                for hh in range(4):
                    for mc in range(2):
                        P("transpose", [kbb, cC], [pbf], out=pbf.ap[:, (hh * 2 + mc) * 128:(hh * 2 + mc + 1) * 128],
                          in_=kbb.ap[:, mc, hh * 128:(hh + 1) * 128], identity=identb, inc=(hh == 3 and mc == 1))
                ktb = KTb[b % 2]
                if b % 2 == 0:
                    V("tensor_copy", [pbf], [ktb], out=ktb.ap, in_=pbf.ap.rearrange("p (a b) -> p a b", a=4))
                else:
                    acopy([pbf], [ktb], ktb.ap, pbf.ap.rearrange("p (a b) -> p a b", a=4))
                for hh in range(4):
                    P("matmul", [qm, ktb], [pssb[hh]], pssb[hh].ap[0:SB, 0:NMEM], lhsT=qm.ap[:, hh, b, :], rhs=ktb.ap[:, hh, :],
                      start=(b == 0), stop=(b == NB - 1), inc=(hh == 3))
            Pns = View(Pn.ap, Pn.cells)
            for hh in range(4):
                acopy([pssb[hh]], [sc_s], sc_s.ap[0:SB, hh, :], pssb[hh].ap[0:SB, 0:NMEM])
            ps_reserved.discard(ipss)
            ps_reserved.discard(ipss2)
            softmax(View(sc_s.ap.rearrange("p h m -> p (h m)"), sc_s.cells), SB, Pns)
            ptp = next_pt()
            pbf = bank_bf(ptp, 0)
            for mc in range(2):
                for hh in range(4):
                    P("transpose", [Pns, cC], [pbf], out=pbf.ap[:, (mc * 4 + hh) * SB:(mc * 4 + hh + 1) * SB],
                      in_=Pns.ap[0:SB, hh, mc * 128:(mc + 1) * 128], identity=identb[0:SB, 0:SB], inc=(mc == 1 and hh == 3))
            V("tensor_copy", [pbf], [PTs], out=PTs.ap, in_=pbf.ap[:, 0:8 * SB].rearrange("p (a b) -> p a b", a=8))
            ipos = next_pt_idx()
            ps_reserved.add(ipos)
            pos_ = bank(pst(ipos), 0)
            for b in range(NB):
                vbb = vb[b % 2]
                T.dma("g", vbb.ap, cv[l, b].rearrange("(mc p) d -> p mc d", p=128), [], [vbb])
                for hh in range(4):
                    for mc in range(2):
                        P("matmul", [vbb, PTs], [pos_], pos_.ap[:, hh * SB + 4 * b:hh * SB + 4 * b + 4], lhsT=vbb.ap[:, mc, hh * 128:(hh + 1) * 128],
                          rhs=PTs.ap[:, mc * 4 + hh, 4 * b:4 * b + 4], start=(mc == 0), stop=(mc == 1), inc=(hh == 3 and mc == 1))
            acopy([pos_], [oT], oT.ap[:, :, TB:TB + SB], pos_.ap[:, 0:4 * SB].rearrange("p (a b) -> p a b", a=4))
            ps_reserved.discard(ipos)

        dump("asmp_b%dl%d" % (blk, l), oT.ap[:, :, :], RR.cells)
        xas = RR.view(26624, BF16, [KC, TMAX])
        ip = next_pt_idx()
        ps_reserved.add(ip)
        pms = pst(ip)
        for m in range(KC):
            if m % 8 == 0:
                wo = wget([(0, w_xo[l].rearrange("(hc p) n -> p hc n", p=128)[:, :, m * 128:m * 128 + 1024], [4, 1024])])
            pt = next_pt()
            for hc in range(4):
                for ni, (c0, n) in enumerate(NT):
                    P("matmul", [wo, oT], [pt], pt.ap[:, c0:c0 + n], lhsT=wo.ap[:, hc * 1024 + (m % 8) * 128:hc * 1024 + (m % 8) * 128 + 128],
                      rhs=oT.ap[:, hc, c0:c0 + n], start=(hc == 0), stop=(hc == 3), inc=(hc == 3 and ni == len(NT) - 1))
            mc_ = [RR.cells[8 + m // 2]]
            acopy([pt], mc_, xas.ap[:, m, 0:TT], pt.ap[:, 0:TT])
            ms_accum(pms, pt.ap[:, 0:TT], [pt], m, KC)
        ps_reserved.discard(ip)
        postnorm_add("g_x_post", l, pms, lambda m: (xas.ap[:, m, 0:TT], [RR.cells[8 + m // 2]]))

    def emit_ffn(blk, l, X, S, TT, NT, prenorm, postnorm_add, proj, s4, ms_accum):
        carryF = carryF_l[l]
        prenorm("g_ffn_pre", l, True)
        oacc = RR.view(0, F32, [KC, TMAX])
        prevF = FF.view(0, F32, [2 * NFF, NB * 2])
        FS = [FF.view(11264 + i * 3584, F32, [896]) for i in range(4)]
        act = [FF.view(25600 + i * 1792, BF16, [896]) for i in range(4)]
        fl = f_conv_w[l].rearrange("k (c p) -> (k c) p", p=128)
        r0_ = 0
        while r0_ < 3 * 2 * NFF:
            n = min(128, 3 * 2 * NFF - r0_)
            st_ = FS[0]
            T.dma("s", st_.ap[0:n, 0:128], fl[r0_:r0_ + n, :], [], [st_])
            pt = next_pt()
            P("transpose", [st_, cC], [pt], out=pt.ap[:, 0:n], in_=st_.ap[0:n, 0:128], identity=identf[0:n, 0:n])
            V("tensor_copy", [pt], [fcw], out=NN.t[:, FCW0 + r0_:FCW0 + r0_ + n], in_=pt.ap[:, 0:n])
            r0_ += n

        def fw(k, c):
            return NN.t[:, FCW0 + k * 2 * NFF + c:FCW0 + k * 2 * NFF + c + 1]

        if X:
            flat = sff[l].rearrange("b r c -> (b r) c")
            for c0 in range(0, 2 * NFF, 7):
                ncch = min(7, 2 * NFF - c0)
                st_ = FS[1 + (c0 // 7) % 2]
                T.dma("s", st_.ap[0:32, 0:ncch * 128], flat[:, c0 * 128:(c0 + ncch) * 128], [], [st_])
                pt = next_pt()
                for i in range(ncch):
                    P("transpose", [st_, cC], [pt], out=pt.ap[:, i * 32:(i + 1) * 32], in_=st_.ap[0:32, i * 128:(i + 1) * 128],
                      identity=identf[0:32, 0:32], inc=(i == ncch - 1))
                acopy([pt], [prevF], prevF.ap[:, c0:c0 + ncch, :], pt.ap[:, 0:ncch * 32].rearrange("p (a b) -> p a b", b=32))

        E = 2 + TB + (96 if X else 0)
        LN = E - 2
        grp = []
        ngroups = (NFF + 1) // 2
        for j in range(NFF):
            wt = wget([(0, w_cols(w_up, l, j * 128, 128), [KC, 128]), (2048, w_cols(w_up, l, DFF + j * 128, 128), [KC, 128])])
            pg = proj(wt, 0, kstride=128)
            pu = proj(wt, 2048, kstride=128)
            eg, eu, ag, au = FS
            for (pp, ee, cidx) in ((pg, eg, j), (pu, eu, NFF + j)):
                acopy([pp], [ee], ee.ap[:, 2:2 + TB], pp.ap[:, 0:TB])
                if blk == 0:
                    V("memset", [], [ee], ee.ap[:, 0:2], 0.0)
                else:
                    V("tensor_copy", [carryF], [ee], out=ee.ap[:, 0:2], in_=carryF.ap[:, cidx, :])
                if X:
                    exs = ee.ap[:, 2 + TB:2 + TB + 96].rearrange("p (b s) -> p b s", s=6)
                    acopy([pp], [ee], exs[:, :, 2:6], s4(pp.ap))
                    V("tensor_copy", [prevF], [ee], out=exs[:, :, 0:2], in_=prevF.ap[:, cidx, :].rearrange("p (b r) -> p b r", r=2))
                V("tensor_copy", [ee], [carryF], out=carryF.ap[:, cidx, :], in_=ee.ap[:, TB:TB + 2])
            for (ee, aa, cidx) in ((eg, ag, j), (eu, au, NFF + j)):
                V("tensor_scalar", [ee, fcw], [aa], out=aa.ap[:, 0:LN], in0=ee.ap[:, 0:LN], scalar1=fw(0, cidx), scalar2=None, op0=ALU.mult)
                V("scalar_tensor_tensor", [ee, aa, fcw], [aa], out=aa.ap[:, 0:LN], in0=ee.ap[:, 1:1 + LN], scalar=fw(1, cidx), in1=aa.ap[:, 0:LN],
                  op0=ALU.mult, op1=ALU.add)
                V("scalar_tensor_tensor", [ee, aa, fcw], [aa], out=aa.ap[:, 0:LN], in0=ee.ap[:, 2:2 + LN], scalar=fw(2, cidx), in1=aa.ap[:, 0:LN],
                  op0=ALU.mult, op1=ALU.add)
            if X:
                for (ee, cidx) in ((eg, j), (eu, NFF + j)):
                    exs = ee.ap[:, 2 + TB:2 + TB + 96].rearrange("p (b s) -> p b s", s=6)
                    V("tensor_copy", [ee], [prevF], out=prevF.ap[:, cidx, :].rearrange("p (b r) -> p b r", r=2), in_=exs[:, :, 4:6])
            A("activation", [ag], [ag], out=ag.ap[:, 0:LN], in_=ag.ap[:, 0:LN], func=AF.Silu)
            ab = act[j % 4]
            V("tensor_tensor", [ag, au], [ab], out=ab.ap[:, 0:LN], in0=ag.ap[:, 0:LN], in1=au.ap[:, 0:LN], op=ALU.mult)
            grp.append((j, ab))
            if len(grp) == 2 or j == NFF - 1:
                j0 = grp[0][0]
                ng = len(grp)
                wd = wget([(0, w_down[l, j0 * 128:(j0 + ng) * 128, :].rearrange("(c p) n -> p c n", p=128), [ng, D])])
                first = (j0 == 0)
                for m in range(KC):
                    pt = next_pt()
                    NT3 = CTX['NT3']
                    for ni, (c0, n) in enumerate(NT3):
                        for ci, (jj, abb) in enumerate(grp):
                            if c0 < TB:
                                rhs = abb.ap[:, c0:c0 + n]
                            else:
                                rhs = abb.ap[:, TB:TB + 96].rearrange("p (b s) -> p b s", s=6)[:, :, 2:6]
                            P("matmul", [wd, abb], [pt], pt.ap[:, c0:c0 + n], lhsT=wd.ap[:, ci * D + m * 128:ci * D + (m + 1) * 128], rhs=rhs,
                              start=(ci == 0), stop=(ci == ng - 1), inc=(ci == ng - 1 and ni == len(NT3) - 1))
                    oc = [RR.cells[m]]
                    if first:
                        V("tensor_copy", [pt], oc, out=oacc.ap[:, m, 0:TT], in_=pt.ap[:, 0:TT])
                    else:
                        V("tensor_tensor", [pt] + oc, oc, out=oacc.ap[:, m, 0:TT], in0=oacc.ap[:, m, 0:TT], in1=pt.ap[:, 0:TT], op=ALU.add)
                grp = []
        ip = next_pt_idx()
        ps_reserved.add(ip)
        pms = pst(ip)
        for m in range(KC):
            ms_accum(pms, oacc.ap[:, m, 0:TT], [RR.cells[m]], m, KC)
        ps_reserved.discard(ip)
        postnorm_add("g_ffn_post", l, pms, lambda m: (oacc.ap[:, m, 0:TT], [RR.cells[m]]))
        if blk == 1:
            for c0 in range(0, 2 * NFF, 64):
                ncch = min(64, 2 * NFF - c0)
                pt = next_pt()
                stg = STS.view(0, F32, [128])
                V("tensor_copy", [carryF], [stg], out=stg.ap[:, 0:2 * ncch].rearrange("p (r c) -> p r c", r=2),
                  in_=carryF.ap[:, c0:c0 + ncch, :].rearrange("p c r -> p r c"))
                P("transpose", [stg, cC], [pt], out=pt.ap[0:ncch * 2, 0:128], in_=stg.ap[:, 0:2 * ncch], identity=identf)
                sto = STO.view(0, F32, [DG])
                acopy([pt], [sto], sto.ap[0:ncch * 2, 0:128], pt.ap[0:ncch * 2, 0:128])
                for r in range(2):
                    dst = o_ff[l][r, c0 * 128:(c0 + ncch) * 128].rearrange("(c p) -> c p", p=128)
                    T.dma("s", dst, sto.ap[r * ncch:(r + 1) * ncch, 0:128], [sto], [])
        if X:
            for c0 in range(0, 2 * NFF, 4):
                ncch = min(4, 2 * NFF - c0)
                pt = next_pt()
                P("transpose", [prevF, cC], [pt], out=pt.ap[0:ncch * 32, 0:128], in_=prevF.ap[:, c0:c0 + ncch, :].rearrange("p c q -> p (c q)"), identity=identf)
                sto = STO.view(0, F32, [DG])
                acopy([pt], [sto], sto.ap[0:ncch * 32, 0:128], pt.ap[0:ncch * 32, 0:128])
                for cc in range(ncch):
                    dst = o_ffs[l][:, :, (c0 + cc) * 128:(c0 + cc + 1) * 128].rearrange("b r p -> (b r) p")
                    T.dma("s", dst, sto.ap[cc * 32:(cc + 1) * 32, 0:128], [sto], [])

    def emit_all():
        ps_rr[0] = 0
        ps_reserved.clear()
        try:
            emit_consts()
            T.dma("s", maskc_ap, maskb[:, :], [], [maskc])
            T.dma("s", invc_v.ap, invc[:, :, :], [], [IV.cells[0]])
            for blk in range(nblocks):
                emit_block(blk)
        except StopEmit:
            pass

    T.plan = True
    emit_all()
    T.plan = False
    emit_all()
    T.finish(None)
    es.close()
    return nc


def prep_inputs(inp, LW=L):
    maps = []
    tri = np.triu(np.ones((128, 128), np.float32))
    ident = np.eye(128, dtype=np.float32)
    wins = np.array([2, 4, 8, 16], np.float32)
    wnames = ["g_mix_pre", "g_mix_post", "g_mem", "g_x_pre", "g_x_post", "g_ffn_pre", "g_ffn_post", "w_in", "w_out",
              "a_norm_g", "a_ws", "a_bs", "b_conv_w", "b_conv_b", "b_gn_g", "b_gn_b", "c_lin", "c_scale", "d_conv_w",
              "w_xq", "w_xk", "w_xv", "w_xo", "w_up", "f_conv_w", "w_down"]
    shared = {k: np.ascontiguousarray(inp[k][:LW], dtype=np.float32) for k in wnames}
    for c in range(8):
        b = c // 2
        second = c % 2
        xr = np.zeros((REG, D), np.float32)
        mask = np.ones((128, HALO), np.float32)
        invc = np.empty((128, 4, 16), np.float32)
        if second:
            xr[:] = inp["x_prompt"][b, OWN - HALO:2 * OWN]
            invc[:] = (1.0 / wins)[None, :, None]
        else:
            xr[HALO:] = inp["x_prompt"][b, 0:OWN]
            mask[:] = 0.0
            pos = np.arange(16, dtype=np.float32)
            invc[:] = (1.0 / np.minimum(pos[None, :] + 1.0, wins[:, None]))[None]
        m = dict(shared)
        m.update(
            xreg=xr,
            xsmp=np.ascontiguousarray(inp["x_sample"][NB * c:NB * (c + 1)].reshape(SB, D)),
            memp=np.ascontiguousarray(inp["mem_prompt"][b]),
            ck=np.ascontiguousarray(inp["cache_mem_k"][:LW, NB * c:NB * (c + 1)]),
            cv=np.ascontiguousarray(inp["cache_mem_v"][:LW, NB * c:NB * (c + 1)]),
            scb=np.ascontiguousarray(inp["state_conv_b"][:LW, NB * c:NB * (c + 1)]),
            spl=np.ascontiguousarray(inp["state_pool"][:LW, NB * c:NB * (c + 1)]),
            ssc=np.ascontiguousarray(inp["state_sconv"][:LW, NB * c:NB * (c + 1)]),
            sff=np.ascontiguousarray(inp["state_ffn_conv"][:LW, NB * c:NB * (c + 1)]),
            maskb=mask, invc=invc, tri=tri, ident=ident,
        )
        maps.append(m)
    return maps


def assemble(res):
    r = res
    B = 4
    yp = np.empty((B, 2 * OWN, D), np.float32)
    ys = np.empty((8 * NB, 4, D), np.float32)
    mk = np.empty((L, B, NMEM, DX), np.float32)
    mv = np.empty((L, B, NMEM, DX), np.float32)
    cb = np.empty((L, B, 30, DG), np.float32)
    pl = np.empty((L, B, 15, DG), np.float32)
    sc = np.empty((L, B, 2, DG), np.float32)
    ff = np.empty((L, B, 2, 2 * DFF), np.float32)
    cbs = np.empty((L, 8 * NB, 30, DG), np.float32)
    pls = np.empty((L, 8 * NB, 15, DG), np.float32)
    scs = np.empty((L, 8 * NB, 2, DG), np.float32)
    ffs = np.empty((L, 8 * NB, 2, 2 * DFF), np.float32)
    vs = np.empty((L, 8 * NB, 4, DG), np.float32)
    for c in range(8):
        b = c // 2
        o = r[c]
        if c % 2 == 0:
            yp[b, 0:OWN] = o["o_y"]
            mk[:, b] = o["o_mk"]
            mv[:, b] = o["o_mv"]
        else:
            yp[b, OWN:] = o["o_y"]
            cb[:, b] = o["o_cb"]
            pl[:, b] = o["o_pl"]
            sc[:, b] = o["o_sc"]
            ff[:, b] = o["o_ff"]
        sl = slice(NB * c, NB * (c + 1))
        ys[sl] = o["o_ys"].reshape(NB, 4, D)
        cbs[:, sl] = o["o_cbs"]
        pls[:, sl] = o["o_pls"]
        scs[:, sl] = o["o_scs"]
        ffs[:, sl] = o["o_ffs"]
        vs[:, sl] = o["o_vs"]
    return (yp, ys, mk, mv, cb, pl, sc, ff, cbs, pls, scs, ffs, vs)


_NC_CACHE = {}


def kernel(**inputs):
    if "nc" not in _NC_CACHE:
        _NC_CACHE["nc"] = build()
    nc = _NC_CACHE["nc"]
    maps = prep_inputs(inputs)
    res = run_bass_kernel_spmd(nc, maps, core_ids=list(range(8)))
    return assemble(res.results)
```

```python
import contextlib
import numpy as np
import concourse.bass as bass
import concourse.mybir as mybir
from concourse.bass_utils import run_bass_kernel_spmd

F32 = mybir.dt.float32
BF16 = mybir.dt.bfloat16
AF = mybir.ActivationFunctionType
ALU = mybir.AluOpType
AX = mybir.AxisListType

L = 4
D = 2048
KC = 16
DIN = 4096
DG = 512
DFF = 5504
NFF = 43
NMEM = 256
DX = 512
TB = 768
SB = 64
TMAX = TB + SB
REG = 1536
HALO = 512
OWN = 1024
NB = 16
EPS = 1e-6
QSCALE = 128 ** -0.5
SELF_WAIT = True


class StopEmit(Exception):
    pass


class Sem:
    def __init__(self, h):
        self.h = h
        self.v = 0


class Cell:
    __slots__ = ("lw", "rd")

    def __init__(self):
        self.lw = None
        self.rd = {}


class View:
    def __init__(self, ap, cells):
        self.ap = ap
        self.cells = cells


class Region:
    def __init__(self, nc, es, name, nbytes, cell_bytes, psum=False):
        self.nbytes = nbytes
        self.cb = cell_bytes
        if psum:
            self.t = es.enter_context(nc.psum_tensor(name, [128, nbytes // 4], F32))
        else:
            self.t = es.enter_context(nc.sbuf_tensor(name, [128, nbytes // 4], F32))
        self.cells = [Cell() for _ in range((nbytes + cell_bytes - 1) // cell_bytes)]

    def view(self, off, dtype, shape, parts=128):
        n = int(np.prod(shape))
        esz = 2 if dtype == BF16 else 4
        assert off % 4 == 0 and off + n * esz <= self.nbytes, (off, n, esz, self.nbytes)
        w0 = off // 4
        w1 = (off + n * esz + 3) // 4
        ap = self.t[0:parts, w0:w1]
        if dtype == BF16:
            ap = ap.bitcast(BF16)[:, 0:n]
        if len(shape) == 2:
            ap = ap.rearrange("p (a b) -> p a b", a=shape[0])
        elif len(shape) == 3:
            ap = ap.rearrange("p (a b c) -> p a b c", a=shape[0], b=shape[1])
        c0 = off // self.cb
        c1 = (off + n * esz - 1) // self.cb
        return View(ap, self.cells[c0:c1 + 1])


def bc_mid(ap2, n):
    a = ap2.ap
    return bass.AP(ap2.tensor, ap2.offset, [list(a[0]), [0, n], list(a[1])])


def bc_last(ap2, n):
    a = ap2.ap
    return bass.AP(ap2.tensor, ap2.offset, [list(a[0]), list(a[1]), [0, n]])


class Tracker:
    NDMA = 40

    def __init__(self, nc, es):
        self.nc = nc
        self.plan = False
        self.engs = {}
        for nm, e in (("p", nc.tensor), ("a", nc.scalar), ("v", nc.vector), ("g", nc.gpsimd), ("s", nc.sync)):
            self.engs[nm] = dict(eng=e, sem=Sem(es.enter_context(nc.semaphore("sem_" + nm))), waited={}, pend=False)
        self.dsl = [Sem(es.enter_context(nc.semaphore("dsem%d" % i))) for i in range(self.NDMA)]
        self.dnext = 0
        self.nops = 0

    @staticmethod
    def _cells(lst):
        out = []
        for x in lst:
            if isinstance(x, View):
                out.extend(x.cells)
            elif isinstance(x, Cell):
                out.append(x)
            else:
                out.extend(x)
        return out

    def _deps(self, E, rc, wc):
        deps = {}

        def need(sv):
            s, v = sv
            if deps.get(s, 0) < v:
                deps[s] = v
        for c in rc:
            if c.lw is not None:
                need(c.lw)
        for c in wc:
            if c.lw is not None:
                need(c.lw)
            for s, v in c.rd.items():
                need((s, v))
        pend = []
        for s, v in deps.items():
            if s is E["sem"]:
                if E is self.engs["p"] or not SELF_WAIT:
                    continue
                if v > s.v:
                    continue
            if E["waited"].get(s, 0) < v:
                pend.append((s, v))
        return pend

    def op(self, en, method, R, W, *args, inc=True, **kw):
        if self.plan:
            return None
        E = self.engs[en]
        rc = self._cells(R)
        wc = self._cells(W)
        pend = self._deps(E, rc, wc)
        for s, v in pend[:-1]:
            E["eng"].wait_ge(s.h, v)
        ins = getattr(E["eng"], method)(*args, **kw)
        if pend:
            ins._wait_ge(pend[-1][0].h, pend[-1][1])
        for s, v in pend:
            E["waited"][s] = v
        sem = E["sem"]
        tag = (sem, sem.v + 1)
        if inc:
            ins.then_inc(sem.h, 1)
            sem.v += 1
            E["pend"] = False
        else:
            E["pend"] = True
        for c in rc:
            c.rd[tag[0]] = tag[1]
        for c in wc:
            c.lw = tag
            c.rd = {}
        self.nops += 1
        return ins

    def dma(self, qn, out, in_, R, W, **kw):
        if self.plan:
            return None
        E = self.engs[qn]
        rc = self._cells(R)
        wc = self._cells(W)
        pend = self._deps(E, rc, wc)
        slot = self.dsl[self.dnext]
        self.dnext = (self.dnext + 1) % self.NDMA
        if slot.v > 0 and E["waited"].get(slot, 0) < slot.v:
            pend.append((slot, slot.v))
        for s, v in pend[:-1]:
            E["eng"].wait_ge(s.h, v)
        ins = E["eng"].dma_start(out=out, in_=in_, **kw)
        if pend:
            ins._wait_ge(pend[-1][0].h, pend[-1][1])
        for s, v in pend:
            E["waited"][s] = v
        ins.then_inc(slot.h, 16)
        slot.v += 16
        tag = (slot, slot.v)
        for c in rc:
            c.rd[slot] = slot.v
        for c in wc:
            c.lw = tag
            c.rd = {}
        return ins

    def finish(self, out_cells):
        S = self.engs["s"]
        for slot in self.dsl:
            if slot.v > 0:
                S["eng"].wait_ge(slot.h, slot.v)
        for nm in ("p", "a", "v", "g"):
            E = self.engs[nm]
            assert not E["pend"], nm
            if E["sem"].v > 0:
                S["eng"].wait_ge(E["sem"].h, E["sem"].v)


def build(depth=L, nblocks=2, stop=None, dbg_shape=None, LW=L):
    nc = bass.Bass("TRN2", target_bir_lowering=False)
    L = LW
    es = contextlib.ExitStack()
    T = Tracker(nc, es)

    def din(name, shape, dt=F32):
        return nc.dram_tensor(name, list(shape), dt, kind="ExternalInput").ap()

    def dout(name, shape, dt=F32):
        return nc.dram_tensor(name, list(shape), dt, kind="ExternalOutput").ap()

    xreg = din("xreg", [REG, D])
    xsmp = din("xsmp", [SB, D])
    memp = din("memp", [NMEM, D])
    ck = din("ck", [L, NB, NMEM, DX])
    cv = din("cv", [L, NB, NMEM, DX])
    scb = din("scb", [L, NB, 30, DG])
    spl = din("spl", [L, NB, 15, DG])
    ssc = din("ssc", [L, NB, 2, DG])
    sff = din("sff", [L, NB, 2, 2 * DFF])
    maskb = din("maskb", [128, HALO])
    invc = din("invc", [128, 4, 16])
    tri = din("tri", [128, 128])
    ident = din("ident", [128, 128])
    gn = {}
    for nm in ("g_mix_pre", "g_mix_post", "g_mem", "g_x_pre", "g_x_post", "g_ffn_pre", "g_ffn_post"):
        gn[nm] = din(nm, [L, D])
    w_in = din("w_in", [L, D, DIN])
    w_out = din("w_out", [L, D, D])
    a_norm_g = din("a_norm_g", [L, DG])
    a_ws = din("a_ws", [L, 4, 128, 128])
    a_bs = din("a_bs", [L, 4, 128])
    b_conv_w = din("b_conv_w", [L, 31, DG])
    b_conv_b = din("b_conv_b", [L, DG])
    b_gn_g = din("b_gn_g", [L, DG])
    b_gn_b = din("b_gn_b", [L, DG])
    c_lin = din("c_lin", [L, 4, 128, 128])
    c_scale = din("c_scale", [L, DG])
    d_conv_w = din("d_conv_w", [L, 3, DG])
    w_xq = din("w_xq", [L, D, DX])
    w_xk = din("w_xk", [L, D, DX])
    w_xv = din("w_xv", [L, D, DX])
    w_xo = din("w_xo", [L, DX, D])
    w_up = din("w_up", [L, D, 2 * DFF])
    f_conv_w = din("f_conv_w", [L, 3, 2 * DFF])
    w_down = din("w_down", [L, DFF, D])

    o_y = dout("o_y", [OWN, D])
    o_ys = dout("o_ys", [SB, D])
    o_mk = dout("o_mk", [L, NMEM, DX])
    o_mv = dout("o_mv", [L, NMEM, DX])
    o_cb = dout("o_cb", [L, 30, DG])
    o_pl = dout("o_pl", [L, 15, DG])
    o_sc = dout("o_sc", [L, 2, DG])
    o_ff = dout("o_ff", [L, 2, 2 * DFF])
    o_cbs = dout("o_cbs", [L, NB, 30, DG])
    o_pls = dout("o_pls", [L, NB, 15, DG])
    o_scs = dout("o_scs", [L, NB, 2, DG])
    o_ffs = dout("o_ffs", [L, NB, 2, 2 * DFF])
    o_vs = dout("o_vs", [L, NB, 4, DG])
    o_dbg = dout("o_dbg", dbg_shape) if dbg_shape else None

    Hh = Region(nc, es, "Hh", KC * TMAX * 4, TMAX * 4)
    XN = Region(nc, es, "XN", KC * TMAX * 2, TMAX * 2)
    RR = Region(nc, es, "RR", KC * TMAX * 4, TMAX * 4)
    FF = Region(nc, es, "FF", 32768, 512)
    NN = Region(nc, es, "NN", 7168, 128)
    CC = Region(nc, es, "CC", 7168, 7168)
    WW = [Region(nc, es, "WW%d" % i, 8192, 4096) for i in range(3)]
    PS = Region(nc, es, "PS", 16384, 2048, psum=True)

    h = Hh.view(0, F32, [KC, TMAX])
    xn = XN.view(0, BF16, [KC, TMAX])

    def hk(kc):
        return Hh.cells[kc]

    def xk(kc):
        return XN.cells[kc]

    CO = {}
    coff = [0]

    def calloc(name, n):
        CO[name] = coff[0]
        coff[0] += n
    for nm in gn:
        calloc(nm, L * KC)
    for nm in ("b_conv_b", "b_gn_g", "b_gn_b", "c_scale"):
        calloc(nm, L * 4)
    calloc("b_conv_w", L * 31 * 4)
    calloc("d_conv_w", L * 3 * 4)
    calloc("identf", 128)
    calloc("tri", 128)
    calloc("identb", 64)
    calloc("onesb", 64)
    calloc("onesf", 128)
    calloc("eps", 1)
    calloc("zero", 1)
    assert coff[0] * 4 <= 7168, coff[0]
    cC = CC.cells[0]

    def cst(name, n, off=0, dt=F32):
        w0 = CO[name] + off
        if dt == BF16:
            return CC.t[:, CO[name]:CO[name] + (n + 1) // 2].bitcast(BF16)[:, off:off + n]
        return CC.t[:, w0:w0 + n]

    identf = cst("identf", 128)
    identb = cst("identb", 128, dt=BF16)
    onesb = cst("onesb", 128, dt=BF16)
    onesf = cst("onesf", 128)
    tri_c = cst("tri", 128)
    eps_c = cst("eps", 1)

    carryB_l = [NN.view(l_ * 1440, F32, [4, 30]) for l_ in range(L)]
    carryC_l = [NN.view(l_ * 1440 + 480, F32, [4, 15]) for l_ in range(L)]
    carryD_l = [NN.view(l_ * 1440 + 720, F32, [4, 2]) for l_ in range(L)]
    carryF_l = [NN.view(l_ * 1440 + 752, F32, [2 * NFF, 2]) for l_ in range(L)]
    smalls = NN.view(5760, F32, [64])
    fcw = NN.view(6016, F32, [3, 2 * NFF])
    FCW0 = 6016 // 4

    ps_rr = [0]
    ps_reserved = set()

    def pst(i):
        return PS.view(i * 4096, F32, [1024])

    def next_pt():
        while True:
            i = ps_rr[0]
            ps_rr[0] = (i + 1) % 4
            if i not in ps_reserved:
                return pst(i)

    def next_pt_idx():
        while True:
            i = ps_rr[0]
            ps_rr[0] = (i + 1) % 4
            if i not in ps_reserved:
                return i

    def bank(v, b):
        return View(v.ap[:, b * 512:(b + 1) * 512], v.cells[b:b + 1])

    def bank_bf(v, b):
        return View(v.ap[:, b * 512:(b + 1) * 512].bitcast(BF16), v.cells[b:b + 1])

    wplan = []
    wstate = dict(issued=0, used=0)

    def wget(parts):
        if T.plan:
            wplan.append(parts)
            return View(WW[0].t[:, :].bitcast(BF16), WW[0].cells)
        i = wstate["used"]
        wstate["used"] += 1
        while wstate["issued"] < min(len(wplan), i + 2):
            k = wstate["issued"]
            reg = WW[k % 3]
            for (eoff, dap, shp) in wplan[k]:
                n = int(np.prod(shp))
                dst = reg.t[:, :].bitcast(BF16)[:, eoff:eoff + n]
                if len(shp) == 2:
                    dst = dst.rearrange("p (a b) -> p a b", a=shp[0])
                c_lo_ = (eoff * 2) // 4096
                c_hi_ = (eoff * 2 + n * 2 - 1) // 4096
                T.dma("g", dst, dap, [], reg.cells[c_lo_:c_hi_ + 1])
            wstate["issued"] += 1
        reg = WW[i % 3]
        return View(reg.t[:, :].bitcast(BF16), reg.cells)

    def w_cols(w, l, c0, n):
        return w[l].rearrange("(kc p) n -> p kc n", p=128)[:, :, c0:c0 + n]

    def P(method, R, W, *a, **k):
        return T.op("p", method, R, W, *a, **k)

    def A(method, R, W, *a, **k):
        return T.op("a", method, R, W, *a, **k)

    def V(method, R, W, *a, **k):
        return T.op("v", method, R, W, *a, **k)

    def G(method, R, W, *a, **k):
        return T.op("g", method, R, W, *a, **k)

    def acopy(R, W, out, in_, scale=None):
        if scale is None:
            return A("activation", R, W, out=out, in_=in_, func=AF.Copy)
        return A("activation", R, W, out=out, in_=in_, func=AF.Copy, scale=scale)

    dbg_state = dict(done=False)
    CTX = {}

    def dump(name, view_ap, cells):
        if stop == name and not T.plan:
            T.dma("g", o_dbg, view_ap, cells, [])
            raise StopEmit()
        if stop == name and T.plan:
            raise StopEmit()

    def load_T(dram2d, nrows, cname, coff0):
        r0 = 0
        while r0 < nrows:
            n = min(128, nrows - r0)
            st = FF.view(0 if (r0 // 128) % 2 == 0 else 512, F32, [128], parts=128)
            T.dma("s", st.ap[0:n, :], dram2d[r0:r0 + n, :], [], [st])
            pt = next_pt()
            P("transpose", [st, cC], [pt], out=pt.ap[:, 0:n], in_=st.ap[0:n, :], identity=identf[0:n, 0:n])
            V("tensor_copy", [pt], [cC], out=cst(cname, n, coff0 + r0), in_=pt.ap[:, 0:n])
            r0 += n

    def emit_consts():
        T.dma("s", identf, ident[:, :], [], [cC])
        T.dma("s", tri_c, tri[:, :], [], [cC])
        V("tensor_copy", [cC], [cC], out=identb, in_=identf)
        V("memset", [], [cC], onesb, 1.0 / D)
        V("memset", [], [cC], onesf, 1.0 / 128)
        V("memset", [], [cC], eps_c, EPS)
        V("memset", [], [cC], cst("zero", 1), 0.0)
        for nm in gn:
            load_T(gn[nm].rearrange("l (k p) -> (l k) p", p=128), L * KC, nm, 0)
        for nm, t in (("b_conv_b", b_conv_b), ("b_gn_g", b_gn_g), ("b_gn_b", b_gn_b), ("c_scale", c_scale)):
            load_T(t.rearrange("l (j p) -> (l j) p", p=128), L * 4, nm, 0)
        load_T(b_conv_w.rearrange("l k (j p) -> (l k j) p", p=128), L * 31 * 4, "b_conv_w", 0)
        load_T(d_conv_w.rearrange("l k (j p) -> (l k j) p", p=128), L * 3 * 4, "d_conv_w", 0)
        V("memset", [], [NN.cells], NN.t[:, 0:1440], 0.0)

    def gcol(nm, l, kc):
        return cst(nm, 1, l * KC + kc)

    def emit_block(blk):
        X = (blk == 0)
        S = SB if X else 0
        TT = TB + S
        NT = [(0, 512), (512, TT - 512)]
        NT3 = [(0, 512), (512, 256)] + ([(768, 64)] if X else [])
        r0 = blk * TB

        def load_tokens(src_rows, ntok, col0, si):
            st = FF.view(si * 8192, F32, [D])
            T.dma("s", st.ap[0:ntok, :], src_rows, [], [st])
            for q in range(4):
                pt = next_pt()
                for i in range(4):
                    kc = q * 4 + i
                    P("transpose", [st, cC], [pt], out=pt.ap[:, i * 128:i * 128 + ntok],
                      in_=st.ap[0:ntok, kc * 128:(kc + 1) * 128], identity=identf[0:ntok, 0:ntok], inc=(i == 3))
                src = pt.ap[:, 0:512].rearrange("p (a b) -> p a b", a=4)[:, :, 0:ntok]
                dst = h.ap[:, q * 4:(q + 1) * 4, col0:col0 + ntok]
                cells = [hk(q * 4 + i) for i in range(4)]
                if q % 2 == 0:
                    acopy([pt], cells, dst, src)
                else:
                    V("tensor_copy", [pt], cells, out=dst, in_=src)
        for tt in range(TB // 128):
            load_tokens(xreg[r0 + tt * 128:r0 + (tt + 1) * 128, :], 128, tt * 128, tt % 2)
        if X:
            load_tokens(xsmp[:, :], SB, TB, 0)
        dump("load%d" % blk, h.ap[:, :, :], Hh.cells)

        sqb = [FF.view(25600, BF16, [TMAX]), FF.view(27264, BF16, [TMAX])]
        rstd = FF.view(28928, F32, [TMAX])
        tmpf = [FF.view(11264, F32, [TMAX]), FF.view(14848, F32, [TMAX])]

        def ms_accum(pms, src_ap, src_cells, idx, n_total):
            sq = sqb[idx % 2]
            A("activation", src_cells, [sq], out=sq.ap[:, 0:TT], in_=src_ap, func=AF.Square)
            for ni, (c0, n) in enumerate(NT):
                P("matmul", [sq, cC], [pms], pms.ap[:, c0:c0 + n], lhsT=onesb, rhs=sq.ap[:, c0:c0 + n],
                  start=(idx == 0), stop=(idx == n_total - 1), inc=(ni == len(NT) - 1))

        def finish_rstd(pms, masked):
            A("activation", [pms, cC], [rstd], out=rstd.ap[:, 0:TT], in_=pms.ap[:, 0:TT], func=AF.Sqrt,
              bias=eps_c, scale=1.0)
            V("reciprocal", [rstd], [rstd], out=rstd.ap[:, 0:TT], in_=rstd.ap[:, 0:TT])
            if masked and X:
                V("tensor_tensor", [rstd, maskc], [rstd], out=rstd.ap[:, 0:HALO], in0=rstd.ap[:, 0:HALO],
                  in1=maskc_ap, op=ALU.mult)

        def prenorm(gname, l, masked):
            i = next_pt_idx()
            ps_reserved.add(i)
            pms = pst(i)
            for kc in range(KC):
                ms_accum(pms, h.ap[:, kc, 0:TT], [hk(kc)], kc, KC)
            finish_rstd(pms, masked)
            ps_reserved.discard(i)
            for kc in range(KC):
                V("scalar_tensor_tensor", [hk(kc), rstd, cC], [xk(kc)], out=xn.ap[:, kc, 0:TT], in0=h.ap[:, kc, 0:TT],
                  scalar=gcol(gname, l, kc), in1=rstd.ap[:, 0:TT], op0=ALU.mult, op1=ALU.mult)

        def postnorm_add(gname, l, pms, src_fn):
            finish_rstd(pms, False)
            for m in range(KC):
                sap, scells = src_fn(m)
                tm = tmpf[m % 2]
                V("scalar_tensor_tensor", scells + [rstd, cC], [tm], out=tm.ap[:, 0:TT], in0=sap,
                  scalar=gcol(gname, l, m), in1=rstd.ap[:, 0:TT], op0=ALU.mult, op1=ALU.mult)
                V("tensor_tensor", [tm, hk(m)], [hk(m)], out=h.ap[:, m, 0:TT], in0=h.ap[:, m, 0:TT],
                  in1=tm.ap[:, 0:TT], op=ALU.add)

        def proj(wt, coff, rhs_view=None, rcells=None, nk=KC, kstride=None):
            pt = next_pt()
            ks = kstride if kstride is not None else 256
            for kc in range(nk):
                for ni, (c0, n) in enumerate(NT):
                    P("matmul", [wt, xk(kc)] if rhs_view is None else [wt] + rcells, [pt], pt.ap[:, c0:c0 + n],
                      lhsT=wt.ap[:, kc * ks + coff:kc * ks + coff + 128],
                      rhs=(xn.ap[:, kc, c0:c0 + n] if rhs_view is None else rhs_view(kc, c0, n)),
                      start=(kc == 0), stop=(kc == nk - 1), inc=(kc == nk - 1 and ni == len(NT) - 1))
            return pt

        def s4(ap, c0=TB):
            return ap[:, c0:c0 + SB].rearrange("p (b i) -> p b i", i=4)

        for l in range(depth):
            CL = 128 * l if X else 0
            NT[:] = [(CL, 512 - CL), (512, TT - 512)]
            NT3[:] = [(CL, 512 - CL), (512, 256)] + ([(768, 64)] if X else [])
            CTX['TT0'] = CL // 128
            CTX['NT3'] = NT3
            emit_layer(blk, l, X, S, TT, NT, prenorm, postnorm_add, proj, s4, ms_accum, sqb, rstd, tmpf)

        def store_tokens(dst_rows, ntok, col0, si):
            st = FF.view(si * 8192, F32, [D])
            for q in range(4):
                pt = next_pt()
                for i in range(4):
                    kc = q * 4 + i
                    P("transpose", [hk(kc), cC], [pt], out=pt.ap[0:ntok, i * 128:(i + 1) * 128],
                      in_=h.ap[:, kc, col0:col0 + ntok], identity=identf, inc=(i == 3))
                if q % 2 == 0:
                    acopy([pt], [st], st.ap[0:ntok, q * 512:(q + 1) * 512], pt.ap[0:ntok, 0:512])
                else:
                    V("tensor_copy", [pt], [st], out=st.ap[0:ntok, q * 512:(q + 1) * 512], in_=pt.ap[0:ntok, 0:512])
            T.dma("s", dst_rows, st.ap[0:ntok, :], [st], [])
        if X:
            for tt in range(4, 6):
                store_tokens(o_y[(tt - 4) * 128:(tt - 3) * 128, :], 128, tt * 128, tt % 2)
            store_tokens(o_ys[:, :], SB, TB, 0)
        else:
            for tt in range(6):
                store_tokens(o_y[256 + tt * 128:256 + (tt + 1) * 128, :], 128, tt * 128, tt % 2)

    MK = Region(nc, es, "MK", HALO * 4, HALO * 4)
    maskc = MK.cells[0]
    maskc_ap = MK.t[:, 0:HALO]
    IV = Region(nc, es, "IV", 64 * 4, 256)
    invc_v = IV.view(0, F32, [4, 16])

    def emit_layer(blk, l, X, S, TT, NT, prenorm, postnorm_add, proj, s4, ms_accum, sqb, rstd, tmpf):
        tag = "b%dl%d" % (blk, l)
        carryB, carryC, carryD, carryF = carryB_l[l], carryC_l[l], carryD_l[l], carryF_l[l]

        def cw(nm, per, idx):
            return cst(nm, 1, l * per + idx)

        prenorm("g_mix_pre", l, True)
        dump("xn_" + tag, xn.ap[:, :, :], XN.cells)

        y = RR.view(0, BF16, [KC, TMAX])

        def yc(i):
            return RR.cells[i // 2]
        SLOT = [RR.view(26624 + i * 6656, F32, [1664]) for i in range(4)]

        prevB = FF.view(7168, F32, [4, NB * 30])
        prevC = FF.view(14848, F32, [4, NB * 15])
        prevD = FF.view(18688, F32, [4, NB * 2])
        if X:
            def load_prev(src, rows_per_b, dstv):
                nrows = NB * rows_per_b
                ntile = (nrows + 127) // 128
                rpt = (nrows + ntile - 1) // ntile
                flat = src[l].rearrange("b r c -> (b r) c")
                pts = [next_pt() for _ in range(2)]
                for ti in range(ntile):
                    a0 = ti * rpt
                    n = min(rpt, nrows - a0)
                    st = FF.view(19200 + (ti % 2) * 2048, F32, [DG])
                    T.dma("s", st.ap[0:n, :], flat[a0:a0 + n, :], [], [st])
                    for j in range(4):
                        pb = bank(pts[j // 2], j % 2)
                        P("transpose", [st, cC], [pb], out=pb.ap[:, a0:a0 + n], in_=st.ap[0:n, j * 128:(j + 1) * 128],
                          identity=identf[0:n, 0:n], inc=(j == 3))
                for j in range(4):
                    pb = bank(pts[j // 2], j % 2)
                    acopy([pb], [dstv], dstv.ap[:, j, 0:nrows], pb.ap[:, 0:nrows])
            load_prev(scb, 30, prevB)
            load_prev(spl, 15, prevC)
            load_prev(ssc, 2, prevD)
            T.dma("s", o_cbs[l, :, 0:26, :], scb[l, :, 4:30, :], [], [])
            T.dma("s", o_pls[l, :, 0:11, :], spl[l, :, 4:15, :], [], [])


        for j in range(4):
            wA = wget([(0, w_cols(w_in, l, 2560 + 128 * j, 128), [KC, 128]), (2048, w_cols(w_in, l, 3072 + 128 * j, 128), [KC, 128])])
            wB = wget([(0, w_cols(w_in, l, 3584 + 128 * j, 128), [KC, 128])])
            px = proj(wA, 0, kstride=128)
            pb_ = proj(wA, 2048, kstride=128)
            pc = proj(wB, 0, kstride=128)
            xs, bs, ext, acc = SLOT
            acopy([px], [xs], xs.ap[:, 0:TT], px.ap[:, 0:TT])
            acopy([pb_], [bs], bs.ap[:, 0:TT], pb_.ap[:, 0:TT])
            if j == 0:
                dump("xs_" + tag, xs.ap[:, 0:TMAX], [xs])
                dump("bs_" + tag, bs.ap[:, 0:TMAX], [bs])
            V("tensor_tensor", [pc, xs], [ext], out=ext.ap[:, 2:2 + TB], in0=pc.ap[:, 0:TB], in1=xs.ap[:, 0:TB], op=ALU.mult)
            if blk == 0:
                V("memset", [], [ext], ext.ap[:, 0:2], 0.0)
            else:
                V("tensor_copy", [carryD], [ext], out=ext.ap[:, 0:2], in_=carryD.ap[:, j, :])
            E = 2 + TB
            if X:
                exs = ext.ap[:, E:E + 96].rearrange("p (b s) -> p b s", s=6)
                V("tensor_tensor", [pc, xs], [ext], out=exs[:, :, 2:6], in0=s4(pc.ap), in1=s4(xs.ap), op=ALU.mult)
                V("tensor_copy", [prevD], [ext], out=exs[:, :, 0:2], in_=prevD.ap[:, j, :].rearrange("p (b r) -> p b r", r=2))
                E += 96
            V("tensor_copy", [ext], [carryD], out=carryD.ap[:, j, :], in_=ext.ap[:, TB:TB + 2])
            LN = E - 2
            dw = lambda k: cst("d_conv_w", 1, (l * 3 + k) * 4 + j)
            V("tensor_scalar", [ext, cC], [acc], out=acc.ap[:, 0:LN], in0=ext.ap[:, 0:LN], scalar1=dw(0), scalar2=None, op0=ALU.mult)
            V("scalar_tensor_tensor", [ext, acc, cC], [acc], out=acc.ap[:, 0:LN], in0=ext.ap[:, 1:1 + LN], scalar=dw(1), in1=acc.ap[:, 0:LN], op0=ALU.mult, op1=ALU.add)
            V("scalar_tensor_tensor", [ext, acc, cC], [acc], out=acc.ap[:, 0:LN], in0=ext.ap[:, 2:2 + LN], scalar=dw(2), in1=acc.ap[:, 0:LN], op0=ALU.mult, op1=ALU.add)
            if j == 0:
                dump("ext_" + tag, ext.ap[:, 0:TMAX + 64], [ext])
                dump("acc_" + tag, acc.ap[:, 0:TMAX + 64], [acc])
            V("tensor_tensor", [acc, bs], [yc(12 + j)], out=y.ap[:, 12 + j, 0:TB], in0=acc.ap[:, 0:TB], in1=bs.ap[:, 0:TB], op=ALU.mult)
            dump("yd%d_" % j + tag, y.ap[:, 12:16, :], RR.cells)
            if X:
                accs = acc.ap[:, TB + 2:TB + 2 + 96].rearrange("p (b s) -> p b s", s=6)[:, :, 0:4]
                V("tensor_tensor", [acc, bs], [yc(12 + j)], out=s4(y.ap[:, 12 + j, :]), in0=accs, in1=s4(bs.ap), op=ALU.mult)
                emit_rows_sample(l, "D", j, ext.ap[:, 2 + TB:2 + TB + 96].rearrange("p (b s) -> p b s", s=6)[:, :, 4:6], [ext], 2)
        if blk == 1:
            emit_rows_prompt(l, o_sc, carryD, 2)
        dump("yd_" + tag, y.ap[:, 12:16, :], RR.cells)

        clin = FF.view(24320, BF16, [4, 128])
        T.dma("g", clin.ap, c_lin[l].rearrange("g c d -> c g d"), [], [clin])
        for g in range(4):
            if g % 2 == 0:
                wA = wget([(0, w_cols(w_in, l, 2048 + 128 * g, 256), [KC, 256])])
            px = proj(wA, (g % 2) * 128)
            ext, s1, s2, pool_s = SLOT
            win = 2 << g
            acopy([px], [ext], ext.ap[:, 15:15 + TB], px.ap[:, 0:TB])
            if blk == 0:
                V("memset", [], [ext], ext.ap[:, 0:15], 0.0)
            else:
                V("tensor_copy", [carryC], [ext], out=ext.ap[:, 0:15], in_=carryC.ap[:, g, :])
            E = 15 + TB
            if X:
                exs = ext.ap[:, E:E + NB * 19].rearrange("p (b s) -> p b s", s=19)
                acopy([px], [ext], exs[:, :, 15:19], s4(px.ap))
                V("tensor_copy", [prevC], [ext], out=exs[:, :, 0:15], in_=prevC.ap[:, g, :].rearrange("p (b r) -> p b r", r=15))
                E += NB * 19
            V("tensor_copy", [ext], [carryC], out=carryC.ap[:, g, :], in_=ext.ap[:, TB:TB + 15])
            cur = ext
            bufs = [s1, s2]
            for st_ in range(g + 1):
                sh = 1 << st_
                nxt = bufs[st_ % 2]
                V("tensor_tensor", [cur], [nxt], out=nxt.ap[:, sh:E], in0=cur.ap[:, sh:E], in1=cur.ap[:, 0:E - sh], op=ALU.add)
                if st_ == 0:
                    V("tensor_copy", [cur], [nxt], out=nxt.ap[:, 0:1], in_=cur.ap[:, 0:1])
                else:
                    V("tensor_copy", [cur], [nxt], out=nxt.ap[:, 0:sh], in_=cur.ap[:, 0:sh])
                cur = nxt
            pooled = View(pool_s.ap.bitcast(BF16), pool_s.cells)
            V("scalar_tensor_tensor", [cur, ext], [pooled], out=pooled.ap[:, 15:E], in0=cur.ap[:, 15:E], scalar=1.0 / win,
              in1=ext.ap[:, 15:E], op0=ALU.mult, op1=ALU.subtract)
            if X:
                tq = FF.view(25344, F32, [16])
                V("tensor_tensor", [cur, IV.cells[0]], [tq], out=tq.ap, in0=cur.ap[:, 15 + HALO:15 + HALO + 16], in1=invc_v.ap[:, g, :], op=ALU.mult)
                V("tensor_tensor", [tq, ext], [pooled], out=pooled.ap[:, 15 + HALO:15 + HALO + 16], in0=tq.ap,
                  in1=ext.ap[:, 15 + HALO:15 + HALO + 16], op=ALU.subtract)
            pt = next_pt()
            NT3 = CTX['NT3']
            for ni, (c0, n) in enumerate(NT3):
                if c0 < TB:
                    rhs = pooled.ap[:, 15 + c0:15 + c0 + n]
                else:
                    rhs = pooled.ap[:, 15 + TB:15 + TB + NB * 19].rearrange("p (b s) -> p b s", s=19)[:, :, 15:19]
                P("matmul", [clin, pooled], [pt], pt.ap[:, c0:c0 + n], lhsT=clin.ap[:, g, :], rhs=rhs, start=True, stop=True,
                  inc=(ni == len(NT3) - 1))
            acopy([pt, cC], [yc(8 + g)], y.ap[:, 8 + g, 0:TT], pt.ap[:, 0:TT], scale=cw("c_scale", 4, g))
            if X:
                emit_rows_sample(l, "C", g, ext.ap[:, 15 + TB:15 + TB + NB * 19].rearrange("p (b s) -> p b s", s=19)[:, :, 15:19], [ext], 4)
        if blk == 1:
            emit_rows_prompt(l, o_pl, carryC, 15)
        dump("yc_" + tag, y.ap[:, 8:12, :], RR.cells)

        for j in range(4):
            wA = wget([(0, w_cols(w_in, l, 1024 + 128 * j, 128), [KC, 128]), (2048, w_cols(w_in, l, 1536 + 128 * j, 128), [KC, 128])])
            pa = proj(wA, 0, kstride=128)
            pg = proj(wA, 2048, kstride=128)
            sg, ext, acc, s3 = SLOT
            A("activation", [pg], [sg], out=sg.ap[:, 0:TT], in_=pg.ap[:, 0:TT], func=AF.Sigmoid)
            V("tensor_tensor", [pa, sg], [ext], out=ext.ap[:, 30:30 + TB], in0=pa.ap[:, 0:TB], in1=sg.ap[:, 0:TB], op=ALU.mult)
            if blk == 0:
                V("memset", [], [ext], ext.ap[:, 0:30], 0.0)
            else:
                V("tensor_copy", [carryB], [ext], out=ext.ap[:, 0:30], in_=carryB.ap[:, j, :])
            E = 30 + TB
            if X:
                exs = ext.ap[:, E:E + NB * 34].rearrange("p (b s) -> p b s", s=34)
                V("tensor_tensor", [pa, sg], [ext], out=exs[:, :, 30:34], in0=s4(pa.ap), in1=s4(sg.ap), op=ALU.mult)
                V("tensor_copy", [prevB], [ext], out=exs[:, :, 0:30], in_=prevB.ap[:, j, :].rearrange("p (b r) -> p b r", r=30))
                E += NB * 34
            V("tensor_copy", [ext], [carryB], out=carryB.ap[:, j, :], in_=ext.ap[:, TB:TB + 30])
            LN = E - 30
            bw = lambda k: cst("b_conv_w", 1, (l * 31 + k) * 4 + j)
            V("tensor_scalar", [ext, cC], [acc], out=acc.ap[:, 0:LN], in0=ext.ap[:, 0:LN], scalar1=bw(0), scalar2=cw("b_conv_b", 4, j),
              op0=ALU.mult, op1=ALU.add)
            for k in range(1, 31):
                V("scalar_tensor_tensor", [ext, acc, cC], [acc], out=acc.ap[:, 0:LN], in0=ext.ap[:, k:k + LN], scalar=bw(k),
                  in1=acc.ap[:, 0:LN], op0=ALU.mult, op1=ALU.add)
            if X:
                emit_rows_sample(l, "B", j, ext.ap[:, 30 + TB:30 + TB + NB * 34].rearrange("p (b s) -> p b s", s=34)[:, :, 30:34], [ext], 4)
            yb = ext
            acopy([acc], [yb], yb.ap[:, 0:TB], acc.ap[:, 0:TB])
            if X:
                acopy([acc], [yb], s4(yb.ap), acc.ap[:, TB + 30:TB + 30 + NB * 34].rearrange("p (b s) -> p b s", s=34)[:, :, 0:4])
            ysq = sg
            A("activation", [yb], [ysq], out=ysq.ap[:, 0:TT], in_=yb.ap[:, 0:TT], func=AF.Square)
            pm = next_pt()
            pq = next_pt()
            for ni, (c0, n) in enumerate(NT):
                P("matmul", [yb, cC], [pm], pm.ap[:, c0:c0 + n], lhsT=onesf, rhs=yb.ap[:, c0:c0 + n], start=True, stop=True, inc=False)
                P("matmul", [ysq, cC], [pq], pq.ap[:, c0:c0 + n], lhsT=onesf, rhs=ysq.ap[:, c0:c0 + n], start=True, stop=True,
                  inc=(ni == len(NT) - 1))
            m2 = acc
            A("activation", [pm], [m2], out=m2.ap[:, 0:TT], in_=pm.ap[:, 0:TT], func=AF.Square)
            V("tensor_tensor", [pq, m2], [m2], out=m2.ap[:, 0:TT], in0=pq.ap[:, 0:TT], in1=m2.ap[:, 0:TT], op=ALU.subtract)
            V("tensor_scalar", [m2], [m2], out=m2.ap[:, 0:TT], in0=m2.ap[:, 0:TT], scalar1=0.0, scalar2=None, op0=ALU.max)
            A("activation", [m2, cC], [m2], out=m2.ap[:, 0:TT], in_=m2.ap[:, 0:TT], func=AF.Sqrt, bias=eps_c, scale=1.0)
            V("reciprocal", [m2], [m2], out=m2.ap[:, 0:TT], in_=m2.ap[:, 0:TT])
            dd = s3
            V("tensor_tensor", [yb, pm], [dd], out=dd.ap[:, 0:TT], in0=yb.ap[:, 0:TT], in1=pm.ap[:, 0:TT], op=ALU.subtract)
            V("tensor_tensor", [dd, m2], [dd], out=dd.ap[:, 0:TT], in0=dd.ap[:, 0:TT], in1=m2.ap[:, 0:TT], op=ALU.mult)
            A("activation", [dd, cC], [yc(4 + j)], out=y.ap[:, 4 + j, 0:TT], in_=dd.ap[:, 0:TT], func=AF.Silu,
              bias=cw("b_gn_b", 4, j), scale=cw("b_gn_g", 4, j))
        if blk == 1:
            emit_rows_prompt(l, o_cb, carryB, 30)
        dump("yb_" + tag, y.ap[:, 4:8, :], RR.cells)

        emit_mixer_a(blk, l, X, S, TT, NT, proj, s4, y, yc, SLOT)
        dump("ya_" + tag, y.ap[:, 0:4, :], RR.cells)

        mixs = RR.view(26624, BF16, [KC, TMAX])
        ip = next_pt_idx()
        ps_reserved.add(ip)
        pms = pst(ip)
        for m in range(KC):
            if m % 2 == 0:
                wt = wget([(0, w_cols(w_out, l, m * 128, 256), [KC, 256])])
            pt = next_pt()
            for kc in range(KC):
                for ni, (c0, n) in enumerate(NT):
                    P("matmul", [wt, yc(kc)], [pt], pt.ap[:, c0:c0 + n], lhsT=wt.ap[:, kc * 256 + (m % 2) * 128:kc * 256 + (m % 2) * 128 + 128],
                      rhs=y.ap[:, kc, c0:c0 + n], start=(kc == 0), stop=(kc == KC - 1), inc=(kc == KC - 1 and ni == len(NT) - 1))
            mc = [RR.cells[8 + m // 2]]
            acopy([pt], mc, mixs.ap[:, m, 0:TT], pt.ap[:, 0:TT])
            ms_accum(pms, pt.ap[:, 0:TT], [pt], m, KC)
        ps_reserved.discard(ip)
        postnorm_add("g_mix_post", l, pms, lambda m: (mixs.ap[:, m, 0:TT], [RR.cells[8 + m // 2]]))
        dump("hmix_" + tag, h.ap[:, :, :], Hh.cells)

        emit_attention(blk, l, X, S, TT, NT, prenorm, postnorm_add, proj, ms_accum)
        dump("hatt_" + tag, h.ap[:, :, :], Hh.cells)

        emit_ffn(blk, l, X, S, TT, NT, prenorm, postnorm_add, proj, s4, ms_accum)
        dump("hffn_" + tag, h.ap[:, :, :], Hh.cells)

    STS = Region(nc, es, "STS", 4 * 64 * 4, 4 * 64 * 4)
    STO = Region(nc, es, "STO", DG * 4, DG * 4)

    def emit_rows_sample(l, which, j, src_ap, src_cells, nr):
        sts = STS.view(0, F32, [4, 64])
        n = NB * nr
        V("tensor_copy", src_cells, [sts], out=sts.ap[:, j, 0:n].rearrange("p (r b) -> p b r", b=NB), in_=src_ap)
        if j < 3:
            return
        pt = next_pt()
        for jj in range(4):
            P("transpose", [sts, cC], [pt], out=pt.ap[0:n, jj * 128:(jj + 1) * 128], in_=sts.ap[:, jj, 0:n], identity=identf,
              inc=(jj == 3))
        sto = STO.view(0, F32, [DG])
        acopy([pt], [sto], sto.ap[0:n, :], pt.ap[0:n, 0:512])
        for r in range(nr):
            if which == "B":
                dst = o_cbs[l, :, 26 + r, :]
            elif which == "C":
                dst = o_pls[l, :, 11 + r, :]
            else:
                dst = o_scs[l, :, r, :]
            T.dma("s", dst, sto.ap[r * NB:(r + 1) * NB, :], [sto], [])

    def emit_rows_prompt(l, dst, carry, nr):
        pt = next_pt()
        for jj in range(4):
            P("transpose", [carry, cC], [pt], out=pt.ap[0:nr, jj * 128:(jj + 1) * 128], in_=carry.ap[:, jj, :], identity=identf,
              inc=(jj == 3))
        sto = STO.view(0, F32, [DG])
        acopy([pt], [sto], sto.ap[0:nr, :], pt.ap[0:nr, 0:512])
        T.dma("s", dst[l], sto.ap[0:nr, :], [sto], [])

    def emit_mixer_a(blk, l, X, S, TT, NT, proj, s4, y, yc, SLOT):
        ntt = TB // 128 + (1 if X else 0)
        vn = FF.view(0, BF16, [7, DG])
        gv = FF.view(19200, F32, [DG])
        bias_bc = FF.view(21248, F32, [4, 128])
        WT = FF.view(23296, BF16, [4, 128])
        vns32 = FF.view(26112, F32, [DG])
        junk = RR.view(26624 + 3 * 6656, BF16, [DG])
        t32 = SLOT[2]
        st = View(smalls.ap, smalls.cells)
        T.dma("s", gv.ap, a_norm_g[l].partition_broadcast(128), [], [gv])
        T.dma("s", bias_bc.ap, a_bs[l].partition_broadcast(128), [], [bias_bc])
        for hd in range(4):
            ws = SLOT[0]
            T.dma("s", ws.ap[:, 0:128], a_ws[l, hd], [], [ws])
            pt = next_pt()
            P("transpose", [ws, cC], [pt], out=pt.ap[:, 0:128], in_=ws.ap[:, 0:128], identity=identf)
            V("tensor_tensor", [pt, cC], [WT], out=WT.ap[:, hd, :], in0=pt.ap[:, 0:128], in1=tri_c, op=ALU.mult)
        if X:
            wt4 = FF.view(25408, F32, [4, 16])
            for hd in range(4):
                T.dma("s", wt4.ap[:, hd, :].rearrange("p (i j) -> p i j", j=4), a_ws[l, hd, 0:4, 0:4].partition_broadcast(128), [], [wt4])
        wv0 = wget([(0, w_cols(w_in, l, 512, 256), [KC, 256])])
        wv1 = wget([(0, w_cols(w_in, l, 768, 256), [KC, 256])])
        for tt in range(CTX['TT0'], ntt):
            ntok = 128 if tt < TB // 128 else SB
            c0 = tt * 128
            ptv = next_pt()
            pv = bank(ptv, 0)
            for hf, wv in enumerate((wv0, wv1)):
                for kc in range(KC):
                    P("matmul", [wv, xk(kc)], [pv], pv.ap[0:ntok, hf * 256:(hf + 1) * 256], lhsT=xn.ap[:, kc, c0:c0 + ntok],
                      rhs=wv.ap[:, kc * 256:(kc + 1) * 256], start=(kc == 0), stop=(kc == KC - 1), inc=(hf == 1 and kc == KC - 1))
            V("memset", [], [st], st.ap[:, 0:2], 0.0)
            A("activation", [pv], [junk, st], out=junk.ap[0:ntok, :], in_=pv.ap[0:ntok, :], func=AF.Copy, accum_out=st.ap[0:ntok, 0:1])
            A("activation", [pv], [junk, st], out=junk.ap[0:ntok, :], in_=pv.ap[0:ntok, :], func=AF.Square, accum_out=st.ap[0:ntok, 1:2])
            V("tensor_scalar", [st], [st], out=st.ap[:, 2:3], in0=st.ap[:, 0:1], scalar1=1.0 / DG, scalar2=None, op0=ALU.mult)
            V("tensor_tensor", [st], [st], out=st.ap[:, 3:4], in0=st.ap[:, 2:3], in1=st.ap[:, 2:3], op=ALU.mult)
            V("scalar_tensor_tensor", [st], [st], out=st.ap[:, 4:5], in0=st.ap[:, 1:2], scalar=1.0 / DG, in1=st.ap[:, 3:4],
              op0=ALU.mult, op1=ALU.subtract)
            V("tensor_scalar", [st], [st], out=st.ap[:, 4:5], in0=st.ap[:, 4:5], scalar1=0.0, scalar2=None, op0=ALU.max)
            A("activation", [st, cC], [st], out=st.ap[:, 5:6], in_=st.ap[:, 4:5], func=AF.Sqrt, bias=eps_c, scale=1.0)
            V("reciprocal", [st], [st], out=st.ap[:, 5:6], in_=st.ap[:, 5:6])
            V("tensor_scalar", [pv, st], [t32], out=t32.ap[0:ntok, 0:DG], in0=pv.ap[0:ntok, :], scalar1=st.ap[0:ntok, 2:3], scalar2=st.ap[0:ntok, 5:6],
              op0=ALU.subtract, op1=ALU.mult)
            V("tensor_tensor", [t32, gv], [vn], out=vn.ap[0:ntok, tt, :], in0=t32.ap[0:ntok, 0:DG], in1=gv.ap[0:ntok, :], op=ALU.mult)
            if ntok == SB:
                V("tensor_tensor", [t32, gv], [vns32], out=vns32.ap[0:ntok, :], in0=t32.ap[0:ntok, 0:DG], in1=gv.ap[0:ntok, :], op=ALU.mult)
                T.dma("s", o_vs[l].rearrange("b i c -> (b i) c"), vns32.ap[0:SB, :], [vns32], [])
        for hd in range(4):
            if hd % 2 == 0:
                wu = wget([(0, w_cols(w_in, l, 128 * hd, 256), [KC, 256])])
            pu = proj(wu, (hd % 2) * 128)
            pz = next_pt()
            for tt in range(CTX['TT0'], TB // 128):
                P("matmul", [vn, WT], [pz], pz.ap[:, tt * 128:(tt + 1) * 128], lhsT=vn.ap[:, tt, hd * 128:(hd + 1) * 128],
                  rhs=WT.ap[:, hd, :], start=True, stop=True, inc=(tt == TB // 128 - 1))
            us, t1 = SLOT[0], SLOT[1]
            acopy([pu], [us], us.ap[:, 0:TT], pu.ap[:, 0:TT])
            V("tensor_tensor", [pz, bias_bc], [t1], out=t1.ap[:, 0:TB].rearrange("p (a b) -> p a b", b=128),
              in0=pz.ap[:, 0:TB].rearrange("p (a b) -> p a b", b=128), in1=bc_mid(bias_bc.ap[:, hd, :], TB // 128), op=ALU.add)
            V("tensor_tensor", [t1, us], [yc(hd)], out=y.ap[:, hd, 0:TB], in0=t1.ap[:, 0:TB], in1=us.ap[:, 0:TB], op=ALU.mult)
            if X:
                pT = next_pt()
                P("transpose", [vns32, cC], [pT], out=pT.ap[:, 0:SB], in_=vns32.ap[0:SB, hd * 128:(hd + 1) * 128], identity=identf[0:SB, 0:SB])
                vT = View(t1.ap[:, TB:TB + SB], t1.cells)
                acopy([pT], [t1], vT.ap, pT.ap[:, 0:SB])
                vT3 = vT.ap.rearrange("p (b i) -> p b i", i=4)
                za = View(t1.ap[:, TB + SB:TB + 2 * SB], t1.cells)
                za3 = za.ap.rearrange("p (b i) -> p b i", i=4)
                for i in range(4):
                    for jq in range(i + 1):
                        wsc = wt4.ap[:, hd, i * 4 + jq:i * 4 + jq + 1]
                        if jq == 0:
                            V("tensor_scalar", [t1, wt4], [t1], out=za3[:, :, i:i + 1], in0=vT3[:, :, jq:jq + 1], scalar1=wsc, scalar2=None, op0=ALU.mult)
                        else:
                            V("scalar_tensor_tensor", [t1, wt4], [t1], out=za3[:, :, i:i + 1], in0=vT3[:, :, jq:jq + 1], scalar=wsc,
                              in1=za3[:, :, i:i + 1], op0=ALU.mult, op1=ALU.add)
                V("tensor_tensor", [t1, bias_bc], [t1], out=za3, in0=za3, in1=bc_mid(bias_bc.ap[:, hd, 0:4], NB), op=ALU.add)
                V("tensor_tensor", [t1, us], [yc(hd)], out=s4(y.ap[:, hd, :]), in0=za3, in1=s4(us.ap), op=ALU.mult)

    def emit_attention(blk, l, X, S, TT, NT, prenorm, postnorm_add, proj, ms_accum):
        prenorm("g_x_pre", l, False)
        qT = RR.view(0, BF16, [4, TMAX])
        oT = RR.view(6656, BF16, [4, TMAX])
        memT = RR.view(13312, BF16, [KC, NMEM])
        qm = RR.view(21504, BF16, [4, NB, SB])
        Pn = RR.view(29696, BF16, [4, NMEM])
        PT = RR.view(31744, BF16, [8, 128])
        kb = [RR.view(33792 + i * 2048, BF16, [2, DX]) for i in range(2)]
        KTb = [RR.view(37888 + i * 2048, BF16, [4, NMEM]) for i in range(2)]
        vb = [RR.view(41984 + i * 2048, BF16, [2, DX]) for i in range(2)]
        PTs = RR.view(48128, BF16, [8, SB])
        memn = FF.view(16384, BF16, [D])
        KT = FF.view(20480, BF16, [4, NMEM])
        Vt = FF.view(22528, BF16, [2, DX])
        kvo = [FF.view(24576, F32, [DX]), FF.view(11264, F32, [DX])]
        st = View(smalls.ap, smalls.cells)

        for hc in range(4):
            if hc % 2 == 0:
                wq = wget([(0, w_cols(w_xq, l, hc * 128, 256), [KC, 256])])
            pq = proj(wq, (hc % 2) * 128)
            A("activation", [pq], [qT], out=qT.ap[:, hc, 0:TT], in_=pq.ap[:, 0:TT], func=AF.Copy, scale=QSCALE)

        dump("aq_b%dl%d" % (blk, l), qT.ap[:, :, :], RR.cells)
        for mt in range(2):
            ms_ = FF.view(mt * 8192, F32, [D])
            T.dma("s", ms_.ap, memp[mt * 128:(mt + 1) * 128, :], [], [ms_])
            V("memset", [], [st], st.ap[:, 8:9], 0.0)
            A("activation", [ms_], [memn, st], out=memn.ap, in_=ms_.ap, func=AF.Square, accum_out=st.ap[:, 8:9])
            V("tensor_scalar", [st], [st], out=st.ap[:, 9:10], in0=st.ap[:, 8:9], scalar1=1.0 / D, scalar2=None, op0=ALU.mult)
            A("activation", [st, cC], [st], out=st.ap[:, 10:11], in_=st.ap[:, 9:10], func=AF.Sqrt, bias=eps_c, scale=1.0)
            V("reciprocal", [st], [st], out=st.ap[:, 10:11], in_=st.ap[:, 10:11])
            A("activation", [ms_, st], [memn], out=memn.ap, in_=ms_.ap, func=AF.Copy, scale=st.ap[:, 10:11])
            for q in range(2):
                pt = next_pt()
                pbf = bank_bf(pt, 0)
                for i in range(8):
                    kc = q * 8 + i
                    P("transpose", [memn, cC], [pbf], out=pbf.ap[:, i * 128:(i + 1) * 128], in_=memn.ap[:, kc * 128:(kc + 1) * 128],
                      identity=identb, inc=(i == 7))
                for i in range(8):
                    kc = q * 8 + i
                    A("activation", [pbf, cC], [memT], out=memT.ap[:, kc, mt * 128:(mt + 1) * 128], in_=pbf.ap[:, i * 128:(i + 1) * 128],
                      func=AF.Copy, scale=gcol("g_mem", l, kc))
        dump("amem_b%dl%d" % (blk, l), memT.ap[:, :, :], RR.cells)
        wks = []
        for h2 in range(2):
            wk = wget([(0, w_cols(w_xk, l, h2 * 256, 256), [KC, 256])])
            wks.append(wk)
            for hh in range(2):
                hc = h2 * 2 + hh
                pt = next_pt()
                pk = bank(pt, 0)
                for kc in range(KC):
                    P("matmul", [wk, memT], [pk], pk.ap[:, 0:NMEM], lhsT=wk.ap[:, kc * 256 + hh * 128:kc * 256 + hh * 128 + 128],
                      rhs=memT.ap[:, kc, :], start=(kc == 0), stop=(kc == KC - 1), inc=(kc == KC - 1))
                acopy([pk], [KT], KT.ap[:, hc, :], pk.ap[:, 0:NMEM])
            if blk == 0:
                for mt in range(2):
                    pt = next_pt()
                    pk = bank(pt, 0)
                    for kc in range(KC):
                        P("matmul", [wk, memT], [pk], pk.ap[:, 0:256], lhsT=memT.ap[:, kc, mt * 128:(mt + 1) * 128],
                          rhs=wk.ap[:, kc * 256:(kc + 1) * 256], start=(kc == 0), stop=(kc == KC - 1), inc=(kc == KC - 1))
                    ko = kvo[mt]
                    acopy([pk], [ko], ko.ap[:, h2 * 256:(h2 + 1) * 256], pk.ap[:, 0:256])
                    if h2 == 1:
                        T.dma("s", o_mk[l, mt * 128:(mt + 1) * 128, :], ko.ap, [ko], [])
        for h2 in range(2):
            wv = wget([(0, w_cols(w_xv, l, h2 * 256, 256), [KC, 256])])
            for mt in range(2):
                pt = next_pt()
                pk = bank(pt, 0)
                for kc in range(KC):
                    P("matmul", [wv, memT], [pk], pk.ap[:, 0:256], lhsT=memT.ap[:, kc, mt * 128:(mt + 1) * 128],
                      rhs=wv.ap[:, kc * 256:(kc + 1) * 256], start=(kc == 0), stop=(kc == KC - 1), inc=(kc == KC - 1))
                if blk == 0:
                    ko = kvo[mt]
                    acopy([pk], [ko], ko.ap[:, h2 * 256:(h2 + 1) * 256], pk.ap[:, 0:256])
                    V("tensor_copy", [ko], [Vt], out=Vt.ap[:, mt, h2 * 256:(h2 + 1) * 256], in_=ko.ap[:, h2 * 256:(h2 + 1) * 256])
                    if h2 == 1:
                        T.dma("s", o_mv[l, mt * 128:(mt + 1) * 128, :], ko.ap, [ko], [])
                else:
                    acopy([pk], [Vt], Vt.ap[:, mt, h2 * 256:(h2 + 1) * 256], pk.ap[:, 0:256])

        dump("akv_b%dl%d" % (blk, l), KT.ap[:, :, :], [KT, Vt])
        def softmax(psc, npart, dstP):
            sc3 = psc.ap[0:npart, :].rearrange("p (h m) -> p h m", h=4)
            V("tensor_reduce", [psc], [st], out=st.ap[0:npart, 16:20], in_=sc3, axis=AX.X, op=ALU.max)
            V("tensor_scalar", [st], [st], out=st.ap[0:npart, 20:24], in0=st.ap[0:npart, 16:20], scalar1=-1.0, scalar2=None, op0=ALU.mult)
            V("memset", [], [st], st.ap[:, 24:28], 0.0)
            for hh in range(4):
                A("activation", [psc, st], [dstP, st], out=dstP.ap[0:npart, hh, :], in_=sc3[:, hh, :], func=AF.Exp,
                  bias=st.ap[0:npart, 20 + hh:21 + hh], scale=1.0, accum_out=st.ap[0:npart, 24 + hh:25 + hh])
            V("reciprocal", [st], [st], out=st.ap[0:npart, 28:32], in_=st.ap[0:npart, 24:28])
            V("tensor_tensor", [dstP, st], [dstP], out=dstP.ap[0:npart, :, :], in0=dstP.ap[0:npart, :, :],
              in1=bc_last(st.ap[0:npart, 28:32], NMEM), op=ALU.mult)

        for tt in range(CTX['TT0'], TB // 128):
            psc = next_pt()
            for hh in range(4):
                P("matmul", [qT, KT], [psc], psc.ap[:, hh * 256:(hh + 1) * 256], lhsT=qT.ap[:, hh, tt * 128:(tt + 1) * 128],
                  rhs=KT.ap[:, hh, :], start=True, stop=True, inc=(hh == 3))
            softmax(psc, 128, Pn)
            ptp = next_pt()
            pbf = bank_bf(ptp, 0)
            for mc in range(2):
                for hh in range(4):
                    P("transpose", [Pn, cC], [pbf], out=pbf.ap[:, (mc * 4 + hh) * 128:(mc * 4 + hh + 1) * 128],
                      in_=Pn.ap[:, hh, mc * 128:(mc + 1) * 128], identity=identb, inc=(mc == 1 and hh == 3))
            V("tensor_copy", [pbf], [PT], out=PT.ap, in_=pbf.ap.rearrange("p (a b) -> p a b", a=8))
            po_t = next_pt()
            po = bank(po_t, 0)
            for hh in range(4):
                for mc in range(2):
                    P("matmul", [Vt, PT], [po], po.ap[:, hh * 128:(hh + 1) * 128], lhsT=Vt.ap[:, mc, hh * 128:(hh + 1) * 128],
                      rhs=PT.ap[:, mc * 4 + hh, :], start=(mc == 0), stop=(mc == 1), inc=(hh == 3 and mc == 1))
            acopy([po], [oT], oT.ap[:, :, tt * 128:(tt + 1) * 128], po.ap.rearrange("p (a b) -> p a b", a=4))

        dump("aprm_b%dl%d" % (blk, l), oT.ap[:, :, :], RR.cells)
        if X:
            G("memset", [], [qm], qm.ap, 0.0)
            for hh in range(4):
                dst = bass.AP(qm.ap.tensor, qm.ap.offset + hh * NB * SB, [list(qm.ap.ap[0]), [SB + 4, NB], [1, 4]])
                V("tensor_copy", [qT], [qm], out=dst, in_=qT.ap[:, hh, TB:TB + SB].rearrange("p (b i) -> p b i", i=4))
            ipss = next_pt_idx()
            ps_reserved.add(ipss)
            ipss2 = next_pt_idx()
            ps_reserved.add(ipss2)
            pss = pst(ipss)
            pss2 = pst(ipss2)
            pssb = [bank(pss, 0), bank(pss, 1), bank(pss2, 0), bank(pss2, 1)]
            sc_s = RR.view(49152, F32, [4, NMEM], parts=128)
            for b in range(NB):
                kbb = kb[b % 2]
                T.dma("g", kbb.ap, ck[l, b].rearrange("(mc p) d -> p mc d", p=128), [], [kbb])
                ptk = next_pt()
                pbf = bank_bf(ptk, 0)
                for hh in range(4):
                    for mc in range(2):
                        P("transpose", [kbb, cC], [pbf], out=pbf.ap[:, (hh * 2 + mc) * 128:(hh * 2 + mc + 1) * 128],
                          in_=kbb.ap[:, mc, hh * 128:(hh + 1) * 128], identity=identb, inc=(hh == 3 and mc == 1))
                ktb = KTb[b % 2]
                if b % 2 == 0:
                    V("tensor_copy", [pbf], [ktb], out=ktb.ap, in_=pbf.ap.rearrange("p (a b) -> p a b", a=4))
                else:
                    acopy([pbf], [ktb], ktb.ap, pbf.ap.rearrange("p (a b) -> p a b", a=4))
                for hh in range(4):
                    P("matmul", [qm, ktb], [pssb[hh]], pssb[hh].ap[0:SB, 0:NMEM], lhsT=qm.ap[:, hh, b, :], rhs=ktb.ap[:, hh, :],
                      start=(b == 0), stop=(b == NB - 1), inc=(hh == 3))
            Pns = View(Pn.ap, Pn.cells)
            for hh in range(4):
                acopy([pssb[hh]], [sc_s], sc_s.ap[0:SB, hh, :], pssb[hh].ap[0:SB, 0:NMEM])
            ps_reserved.discard(ipss)
            ps_reserved.discard(ipss2)
            softmax(View(sc_s.ap.rearrange("p h m -> p (h m)"), sc_s.cells), SB, Pns)
            ptp = next_pt()
            pbf = bank_bf(ptp, 0)
            for mc in range(2):
                for hh in range(4):
                    P("transpose", [Pns, cC], [pbf], out=pbf.ap[:, (mc * 4 + hh) * SB:(mc * 4 + hh + 1) * SB],
                      in_=Pns.ap[0:SB, hh, mc * 128:(mc + 1) * 128], identity=identb[0:SB, 0:SB], inc=(mc == 1 and hh == 3))
            V("tensor_copy", [pbf], [PTs], out=PTs.ap, in_=pbf.ap[:, 0:8 * SB].rearrange("p (a b) -> p a b", a=8))
            ipos = next_pt_idx()
            ps_reserved.add(ipos)
            pos_ = bank(pst(ipos), 0)
            for b in range(NB):
                vbb = vb[b % 2]
                T.dma("g", vbb.ap, cv[l, b].rearrange("(mc p) d -> p mc d", p=128), [], [vbb])
                for hh in range(4):
                    for mc in range(2):
                        P("matmul", [vbb, PTs], [pos_], pos_.ap[:, hh * SB + 4 * b:hh * SB + 4 * b + 4], lhsT=vbb.ap[:, mc, hh * 128:(hh + 1) * 128],
                          rhs=PTs.ap[:, mc * 4 + hh, 4 * b:4 * b + 4], start=(mc == 0), stop=(mc == 1), inc=(hh == 3 and mc == 1))
            acopy([pos_], [oT], oT.ap[:, :, TB:TB + SB], pos_.ap[:, 0:4 * SB].rearrange("p (a b) -> p a b", a=4))
            ps_reserved.discard(ipos)

        dump("asmp_b%dl%d" % (blk, l), oT.ap[:, :, :], RR.cells)
        xas = RR.view(26624, BF16, [KC, TMAX])
        ip = next_pt_idx()
        ps_reserved.add(ip)
        pms = pst(ip)
        for m in range(KC):
            if m % 8 == 0:
                wo = wget([(0, w_xo[l].rearrange("(hc p) n -> p hc n", p=128)[:, :, m * 128:m * 128 + 1024], [4, 1024])])
            pt = next_pt()
            for hc in range(4):
                for ni, (c0, n) in enumerate(NT):
                    P("matmul", [wo, oT], [pt], pt.ap[:, c0:c0 + n], lhsT=wo.ap[:, hc * 1024 + (m % 8) * 128:hc * 1024 + (m % 8) * 128 + 128],
                      rhs=oT.ap[:, hc, c0:c0 + n], start=(hc == 0), stop=(hc == 3), inc=(hc == 3 and ni == len(NT) - 1))
            mc_ = [RR.cells[8 + m // 2]]
            acopy([pt], mc_, xas.ap[:, m, 0:TT], pt.ap[:, 0:TT])
            ms_accum(pms, pt.ap[:, 0:TT], [pt], m, KC)
        ps_reserved.discard(ip)
        postnorm_add("g_x_post", l, pms, lambda m: (xas.ap[:, m, 0:TT], [RR.cells[8 + m // 2]]))

    def emit_ffn(blk, l, X, S, TT, NT, prenorm, postnorm_add, proj, s4, ms_accum):
        carryF = carryF_l[l]
        prenorm("g_ffn_pre", l, True)
        oacc = RR.view(0, F32, [KC, TMAX])
        prevF = FF.view(0, F32, [2 * NFF, NB * 2])
        zc = cst("zero", 1)
        zero2 = bass.AP(zc.tensor, zc.offset, [list(zc.ap[0]), [0, 2]])
        FS = [FF.view(11264 + i * 3584, F32, [896]) for i in range(4)]
        act = [FF.view(25600 + i * 1792, BF16, [896]) for i in range(4)]
        fl = f_conv_w[l].rearrange("k (c p) -> (k c) p", p=128)
        r0_ = 0
        while r0_ < 3 * 2 * NFF:
            n = min(128, 3 * 2 * NFF - r0_)
            st_ = FS[0]
            T.dma("s", st_.ap[0:n, 0:128], fl[r0_:r0_ + n, :], [], [st_])
            pt = next_pt()
            P("transpose", [st_, cC], [pt], out=pt.ap[:, 0:n], in_=st_.ap[0:n, 0:128], identity=identf[0:n, 0:n])
            V("tensor_copy", [pt], [fcw], out=NN.t[:, FCW0 + r0_:FCW0 + r0_ + n], in_=pt.ap[:, 0:n])
            r0_ += n

        def fw(k, c):
            return NN.t[:, FCW0 + k * 2 * NFF + c:FCW0 + k * 2 * NFF + c + 1]

        if X:
            flat = sff[l].rearrange("b r c -> (b r) c")
            for c0 in range(0, 2 * NFF, 7):
                ncch = min(7, 2 * NFF - c0)
                st_ = FS[1 + (c0 // 7) % 2]
                T.dma("s", st_.ap[0:32, 0:ncch * 128], flat[:, c0 * 128:(c0 + ncch) * 128], [], [st_])
                pt = next_pt()
                for i in range(ncch):
                    P("transpose", [st_, cC], [pt], out=pt.ap[:, i * 32:(i + 1) * 32], in_=st_.ap[0:32, i * 128:(i + 1) * 128],
                      identity=identf[0:32, 0:32], inc=(i == ncch - 1))
                acopy([pt], [prevF], prevF.ap[:, c0:c0 + ncch, :], pt.ap[:, 0:ncch * 32].rearrange("p (a b) -> p a b", b=32))

        E = 2 + TB + (96 if X else 0)
        LN = E - 2
        grp = []
        ngroups = (NFF + 1) // 2
        for j in range(NFF):
            wt = wget([(0, w_cols(w_up, l, j * 128, 128), [KC, 128]), (2048, w_cols(w_up, l, DFF + j * 128, 128), [KC, 128])])
            pg = proj(wt, 0, kstride=128)
            pu = proj(wt, 2048, kstride=128)
            eg, eu, ag, au = FS
            for (pp, ee, cidx) in ((pg, eg, j), (pu, eu, NFF + j)):
                acopy([pp], [ee], ee.ap[:, 2:2 + TB], pp.ap[:, 0:TB])
                if blk == 0:
                    acopy([cC], [ee], ee.ap[:, 0:2], zero2)
                else:
                    acopy([carryF], [ee], ee.ap[:, 0:2], carryF.ap[:, cidx, :])
                if X:
                    exs = ee.ap[:, 2 + TB:2 + TB + 96].rearrange("p (b s) -> p b s", s=6)
                    acopy([pp], [ee], exs[:, :, 2:6], s4(pp.ap))
                    acopy([prevF], [ee], exs[:, :, 0:2], prevF.ap[:, cidx, :].rearrange("p (b r) -> p b r", r=2))
                acopy([ee], [carryF], carryF.ap[:, cidx, :], ee.ap[:, TB:TB + 2])
            for (ee, aa, cidx) in ((eg, ag, j), (eu, au, NFF + j)):
                V("tensor_scalar", [ee, fcw], [aa], out=aa.ap[:, 0:LN], in0=ee.ap[:, 0:LN], scalar1=fw(0, cidx), scalar2=None, op0=ALU.mult)
                V("scalar_tensor_tensor", [ee, aa, fcw], [aa], out=aa.ap[:, 0:LN], in0=ee.ap[:, 1:1 + LN], scalar=fw(1, cidx), in1=aa.ap[:, 0:LN],
                  op0=ALU.mult, op1=ALU.add)
                V("scalar_tensor_tensor", [ee, aa, fcw], [aa], out=aa.ap[:, 0:LN], in0=ee.ap[:, 2:2 + LN], scalar=fw(2, cidx), in1=aa.ap[:, 0:LN],
                  op0=ALU.mult, op1=ALU.add)
            if X:
                for (ee, cidx) in ((eg, j), (eu, NFF + j)):
                    exs = ee.ap[:, 2 + TB:2 + TB + 96].rearrange("p (b s) -> p b s", s=6)
                    acopy([ee], [prevF], prevF.ap[:, cidx, :].rearrange("p (b r) -> p b r", r=2), exs[:, :, 4:6])
            A("activation", [ag], [ag], out=ag.ap[:, 0:LN], in_=ag.ap[:, 0:LN], func=AF.Silu)
            ab = act[j % 4]
            V("tensor_tensor", [ag, au], [ab], out=ab.ap[:, 0:LN], in0=ag.ap[:, 0:LN], in1=au.ap[:, 0:LN], op=ALU.mult)
            grp.append((j, ab))
            if len(grp) == 4 or j == NFF - 1:
                j0 = grp[0][0]
                ng = len(grp)
                wds = []
                for q in range(0, ng, 2):
                    nq = min(2, ng - q)
                    wds.append(wget([(0, w_down[l, (j0 + q) * 128:(j0 + q + nq) * 128, :].rearrange("(c p) n -> p c n", p=128), [nq, D])]))
                first = (j0 == 0)
                for m in range(KC):
                    pt = next_pt()
                    NT3 = CTX['NT3']
                    for ni, (c0, n) in enumerate(NT3):
                        for ci, (jj, abb) in enumerate(grp):
                            if c0 < TB:
                                rhs = abb.ap[:, c0:c0 + n]
                            else:
                                rhs = abb.ap[:, TB:TB + 96].rearrange("p (b s) -> p b s", s=6)[:, :, 2:6]
                            wd = wds[ci // 2]
                            P("matmul", [wd, abb], [pt], pt.ap[:, c0:c0 + n], lhsT=wd.ap[:, (ci % 2) * D + m * 128:(ci % 2) * D + (m + 1) * 128], rhs=rhs,
                              start=(ci == 0), stop=(ci == ng - 1), inc=(ci == ng - 1 and ni == len(NT3) - 1))
                    oc = [RR.cells[m]]
                    if first:
                        V("tensor_copy", [pt], oc, out=oacc.ap[:, m, 0:TT], in_=pt.ap[:, 0:TT])
                    else:
                        V("tensor_tensor", [pt] + oc, oc, out=oacc.ap[:, m, 0:TT], in0=oacc.ap[:, m, 0:TT], in1=pt.ap[:, 0:TT], op=ALU.add)
                grp = []
        ip = next_pt_idx()
        ps_reserved.add(ip)
        pms = pst(ip)
        for m in range(KC):
            ms_accum(pms, oacc.ap[:, m, 0:TT], [RR.cells[m]], m, KC)
        ps_reserved.discard(ip)
        postnorm_add("g_ffn_post", l, pms, lambda m: (oacc.ap[:, m, 0:TT], [RR.cells[m]]))
        if blk == 1:
            for c0 in range(0, 2 * NFF, 64):
                ncch = min(64, 2 * NFF - c0)
                pt = next_pt()
                stg = STS.view(0, F32, [128])
                V("tensor_copy", [carryF], [stg], out=stg.ap[:, 0:2 * ncch].rearrange("p (r c) -> p r c", r=2),
                  in_=carryF.ap[:, c0:c0 + ncch, :].rearrange("p c r -> p r c"))
                P("transpose", [stg, cC], [pt], out=pt.ap[0:ncch * 2, 0:128], in_=stg.ap[:, 0:2 * ncch], identity=identf)
                sto = STO.view(0, F32, [DG])
                acopy([pt], [sto], sto.ap[0:ncch * 2, 0:128], pt.ap[0:ncch * 2, 0:128])
                for r in range(2):
                    dst = o_ff[l][r, c0 * 128:(c0 + ncch) * 128].rearrange("(c p) -> c p", p=128)
                    T.dma("s", dst, sto.ap[r * ncch:(r + 1) * ncch, 0:128], [sto], [])
        if X:
            for c0 in range(0, 2 * NFF, 4):
                ncch = min(4, 2 * NFF - c0)
                pt = next_pt()
                P("transpose", [prevF, cC], [pt], out=pt.ap[0:ncch * 32, 0:128], in_=prevF.ap[:, c0:c0 + ncch, :].rearrange("p c q -> p (c q)"), identity=identf)
                sto = STO.view(0, F32, [DG])
                acopy([pt], [sto], sto.ap[0:ncch * 32, 0:128], pt.ap[0:ncch * 32, 0:128])
                for cc in range(ncch):
                    dst = o_ffs[l][:, :, (c0 + cc) * 128:(c0 + cc + 1) * 128].rearrange("b r p -> (b r) p")
                    T.dma("s", dst, sto.ap[cc * 32:(cc + 1) * 32, 0:128], [sto], [])

    def emit_all():
        ps_rr[0] = 0
        ps_reserved.clear()
        try:
            emit_consts()
            T.dma("s", maskc_ap, maskb[:, :], [], [maskc])
            T.dma("s", invc_v.ap, invc[:, :, :], [], [IV.cells[0]])
            for blk in range(nblocks):
                emit_block(blk)
        except StopEmit:
            pass

    T.plan = True
    emit_all()
    T.plan = False
    emit_all()
    T.finish(None)
    es.close()
    return nc


def prep_inputs(inp, LW=L):
    maps = []
    tri = np.triu(np.ones((128, 128), np.float32))
    ident = np.eye(128, dtype=np.float32)
    wins = np.array([2, 4, 8, 16], np.float32)
    wnames = ["g_mix_pre", "g_mix_post", "g_mem", "g_x_pre", "g_x_post", "g_ffn_pre", "g_ffn_post", "w_in", "w_out",
              "a_norm_g", "a_ws", "a_bs", "b_conv_w", "b_conv_b", "b_gn_g", "b_gn_b", "c_lin", "c_scale", "d_conv_w",
              "w_xq", "w_xk", "w_xv", "w_xo", "w_up", "f_conv_w", "w_down"]
    shared = {k: np.ascontiguousarray(inp[k][:LW], dtype=np.float32) for k in wnames}
    for c in range(8):
        b = c // 2
        second = c % 2
        xr = np.zeros((REG, D), np.float32)
        mask = np.ones((128, HALO), np.float32)
        invc = np.empty((128, 4, 16), np.float32)
        if second:
            xr[:] = inp["x_prompt"][b, OWN - HALO:2 * OWN]
            invc[:] = (1.0 / wins)[None, :, None]
        else:
            xr[HALO:] = inp["x_prompt"][b, 0:OWN]
            mask[:] = 0.0
            pos = np.arange(16, dtype=np.float32)
            invc[:] = (1.0 / np.minimum(pos[None, :] + 1.0, wins[:, None]))[None]
        m = dict(shared)
        m.update(
            xreg=xr,
            xsmp=np.ascontiguousarray(inp["x_sample"][NB * c:NB * (c + 1)].reshape(SB, D)),
            memp=np.ascontiguousarray(inp["mem_prompt"][b]),
            ck=np.ascontiguousarray(inp["cache_mem_k"][:LW, NB * c:NB * (c + 1)]),
            cv=np.ascontiguousarray(inp["cache_mem_v"][:LW, NB * c:NB * (c + 1)]),
            scb=np.ascontiguousarray(inp["state_conv_b"][:LW, NB * c:NB * (c + 1)]),
            spl=np.ascontiguousarray(inp["state_pool"][:LW, NB * c:NB * (c + 1)]),
            ssc=np.ascontiguousarray(inp["state_sconv"][:LW, NB * c:NB * (c + 1)]),
            sff=np.ascontiguousarray(inp["state_ffn_conv"][:LW, NB * c:NB * (c + 1)]),
            maskb=mask, invc=invc, tri=tri, ident=ident,
        )
        maps.append(m)
    return maps


def assemble(res):
    r = res
    B = 4
    yp = np.empty((B, 2 * OWN, D), np.float32)
    ys = np.empty((8 * NB, 4, D), np.float32)
    mk = np.empty((L, B, NMEM, DX), np.float32)
    mv = np.empty((L, B, NMEM, DX), np.float32)
    cb = np.empty((L, B, 30, DG), np.float32)
    pl = np.empty((L, B, 15, DG), np.float32)
    sc = np.empty((L, B, 2, DG), np.float32)
    ff = np.empty((L, B, 2, 2 * DFF), np.float32)
    cbs = np.empty((L, 8 * NB, 30, DG), np.float32)
    pls = np.empty((L, 8 * NB, 15, DG), np.float32)
    scs = np.empty((L, 8 * NB, 2, DG), np.float32)
    ffs = np.empty((L, 8 * NB, 2, 2 * DFF), np.float32)
    vs = np.empty((L, 8 * NB, 4, DG), np.float32)
    for c in range(8):
        b = c // 2
        o = r[c]
        if c % 2 == 0:
            yp[b, 0:OWN] = o["o_y"]
            mk[:, b] = o["o_mk"]
            mv[:, b] = o["o_mv"]
        else:
            yp[b, OWN:] = o["o_y"]
            cb[:, b] = o["o_cb"]
            pl[:, b] = o["o_pl"]
            sc[:, b] = o["o_sc"]
            ff[:, b] = o["o_ff"]
        sl = slice(NB * c, NB * (c + 1))
        ys[sl] = o["o_ys"].reshape(NB, 4, D)
        cbs[:, sl] = o["o_cbs"]
        pls[:, sl] = o["o_pls"]
        scs[:, sl] = o["o_scs"]
        ffs[:, sl] = o["o_ffs"]
        vs[:, sl] = o["o_vs"]
    return (yp, ys, mk, mv, cb, pl, sc, ff, cbs, pls, scs, ffs, vs)


_NC_CACHE = {}


def kernel(**inputs):
    if "nc" not in _NC_CACHE:
        _NC_CACHE["nc"] = build()
    nc = _NC_CACHE["nc"]
    maps = prep_inputs(inputs)
    res = run_bass_kernel_spmd(nc, maps, core_ids=list(range(8)))
    return assemble(res.results)
```
